# Optimizing a Trainium2 kernel written in Bass

```python
import math
import jax, jax.numpy as jnp
from jax import lax
import numpy as np

D_MODEL = 2048
BATCH = 4
SEQ = 8192
DEPTH = 1
DEC_BATCH = 8
DEC_SEQ = 16
PAST_LEN = 4096

CHUNK = 64
GDN_HEADS = 8
GDN_DK = 128
GDN_DV = 128
CONV_W = 4
RET_HEADS = 4
RET_DK = 256
RET_DV = 256
GDN_QK = GDN_HEADS * GDN_DK
GDN_V = GDN_HEADS * GDN_DV
GDN_CONV_DIM = 2 * GDN_QK + GDN_V
RET_QK = RET_HEADS * RET_DK
RET_V = RET_HEADS * RET_DV
MIX_WIDTH = GDN_V + RET_V
PROJ_WIDTH = GDN_CONV_DIM + GDN_V + 2 * GDN_HEADS + 2 * RET_QK + 2 * RET_V
D_FF = -(-8 * D_MODEL // (3 * 256)) * 256
RMS_EPS = 1e-6
GN_EPS = 1e-5
L2_EPS = 1e-6
ROPE_BASE = 10000.0

kernel_name = 'hybrid_gdn_retention_stream_step'


def _rmsnorm(x, w):
    xf = x.astype(jnp.float32)
    y = xf * lax.rsqrt(jnp.mean(xf * xf, axis=-1, keepdims=True) + RMS_EPS)
    return (y * w.astype(jnp.float32)).astype(x.dtype)


def _l2norm(x):
    xf = x.astype(jnp.float32)
    return xf * lax.rsqrt(jnp.sum(xf * xf, axis=-1, keepdims=True) + L2_EPS)


def _causal_conv(x, buf, w):
    L = x.shape[1]
    xp = jnp.concatenate([buf.astype(x.dtype), x], axis=1)
    y = sum(xp[:, j:j + L] * w[j] for j in range(CONV_W))
    return jax.nn.silu(y), xp[:, -(CONV_W - 1):]


def _rotary(x, pos):
    d = x.shape[-1]
    inv = 1.0 / (ROPE_BASE ** jnp.linspace(0.0, 1.0, d // 2, dtype=jnp.float32))
    ang = pos.astype(jnp.float32)[:, None] * inv[None, :]
    cos = jnp.cos(ang)[None, :, None, :]
    sin = jnp.sin(ang)[None, :, None, :]
    xf = x.astype(jnp.float32).reshape(*x.shape[:-1], d // 2, 2)
    x0, x1 = xf[..., 0], xf[..., 1]
    return jnp.stack([x0 * cos - x1 * sin, x1 * cos + x0 * sin], axis=-1).reshape(x.shape)


def _chunked_linear_attention(q, k, v, g, s0, beta=None):
    B, H, L, dk = q.shape
    dv = v.shape[-1]
    c = min(CHUNK, L)
    n = L // c

    def chunks(t):
        return jnp.moveaxis(t.reshape(B, H, n, c, *t.shape[3:]), 2, 0)

    q, k, v, g = chunks(q), chunks(k), chunks(v), chunks(g)
    gc = jnp.cumsum(g, axis=-1)
    causal = jnp.tril(jnp.ones((c, c), dtype=bool))
    diff = gc[..., :, None] - gc[..., None, :]
    dmat = jnp.where(causal, jnp.exp(jnp.where(causal, diff, 0.0)), 0.0)
    attn = jnp.einsum('nbhid,nbhjd->nbhij', q, k) * dmat
    delta = beta is not None
    if delta:
        beta = chunks(beta)
        kb = k * beta[..., None]
        strict = jnp.tril(jnp.ones((c, c), dtype=bool), -1)
        a = jnp.where(strict, jnp.einsum('nbhid,nbhjd->nbhij', kb, k) * dmat, 0.0)
        eye = jnp.eye(c, dtype=jnp.float32)
        t_inv = lax.linalg.triangular_solve(a + eye, jnp.broadcast_to(eye, a.shape),
                                            left_side=True, lower=True, unit_diagonal=True)
        u0 = jnp.einsum('nbhij,nbhje->nbhie', t_inv, v * beta[..., None])
        kcd = jnp.einsum('nbhij,nbhjd->nbhid', t_inv, kb * jnp.exp(gc)[..., None])
    else:
        u0 = v
    q_dec = q * jnp.exp(gc)[..., None]
    k_dec = k * jnp.exp(gc[..., -1:] - gc)[..., None]
    g_last = jnp.exp(gc[..., -1])
    xs = (q_dec, k_dec, attn, u0, g_last) + ((kcd,) if delta else ())

    def step(s, xs_i):
        qd, kd, at, u, gl = xs_i[:5]
        if delta:
            u = u - jnp.einsum('bhcd,bhde->bhce', xs_i[5], s)
        o = jnp.einsum('bhcd,bhde->bhce', qd, s) + jnp.einsum('bhij,bhje->bhie', at, u)
        s = s * gl[..., None, None] + jnp.einsum('bhcd,bhce->bhde', kd, u)
        return s, o

    s_end, o = lax.scan(step, s0, xs)
    o = jnp.moveaxis(o, 0, 2).reshape(B, H, L, dv)
    return o, s_end


def _layer(x, pos, conv_buf, s_gdn, s_ret, attn_norm_w, w_in, conv_w, a_log, dt_bias,
           gdn_norm_w, ret_gn_w, w_out, ffn_norm_w, w_gate_up, w_down):
    f32 = jnp.float32
    B, L, _ = x.shape
    h = _rmsnorm(x, attn_norm_w)
    p = h @ w_in
    cuts = np.cumsum([GDN_CONV_DIM, GDN_V, GDN_HEADS, GDN_HEADS, RET_QK, RET_QK, RET_V]).tolist()
    qkv_a, z_a, b_a, a_a, q_r, k_r, v_r, g_r = jnp.split(p, cuts, axis=-1)

    def heads(t, nh):
        return jnp.swapaxes(t.reshape(B, L, nh, -1), 1, 2)

    qkv_a, new_buf = _causal_conv(qkv_a, conv_buf, conv_w)
    q_a, k_a, v_a = jnp.split(qkv_a, [GDN_QK, 2 * GDN_QK], axis=-1)
    q_a = _l2norm(heads(q_a, GDN_HEADS)) * (GDN_DK ** -0.5)
    k_a = _l2norm(heads(k_a, GDN_HEADS))
    v_a = heads(v_a, GDN_HEADS).astype(f32)
    beta = jnp.swapaxes(jax.nn.sigmoid(b_a.astype(f32)), 1, 2)
    g_a = -jnp.exp(a_log.astype(f32)) * jax.nn.softplus(a_a.astype(f32) + dt_bias.astype(f32))
    g_a = jnp.swapaxes(g_a, 1, 2)
    o_a, s_gdn_new = _chunked_linear_attention(q_a, k_a, v_a, g_a, s_gdn.astype(f32), beta)
    o_a = jnp.swapaxes(o_a, 1, 2)
    o_a = o_a * lax.rsqrt(jnp.mean(o_a * o_a, axis=-1, keepdims=True) + RMS_EPS) * gdn_norm_w.astype(f32)
    o_a = o_a * jax.nn.silu(z_a.astype(f32)).reshape(B, L, GDN_HEADS, GDN_DV)

    q_r = _rotary(q_r.reshape(B, L, RET_HEADS, RET_DK), pos)
    k_r = _rotary(k_r.reshape(B, L, RET_HEADS, RET_DK), pos) * (RET_DK ** -0.5)
    v_r = v_r.reshape(B, L, RET_HEADS, RET_DV).astype(f32)
    log_gamma = jnp.log(1.0 - 2.0 ** (-5.0 - jnp.arange(RET_HEADS, dtype=f32)))
    g_ret = jnp.broadcast_to(log_gamma[None, :, None], (B, RET_HEADS, L))
    o_r, s_ret_new = _chunked_linear_attention(jnp.swapaxes(q_r, 1, 2), jnp.swapaxes(k_r, 1, 2),
                                               jnp.swapaxes(v_r, 1, 2), g_ret, s_ret.astype(f32))
    o_r = jnp.swapaxes(o_r, 1, 2)
    mu = jnp.mean(o_r, axis=-1, keepdims=True)
    var = jnp.mean(jnp.square(o_r - mu), axis=-1, keepdims=True)
    o_r = (o_r - mu) * lax.rsqrt(var + GN_EPS) * ret_gn_w.astype(f32).reshape(RET_HEADS, RET_DV)
    o_r = o_r * jax.nn.silu(g_r.astype(f32)).reshape(B, L, RET_HEADS, RET_DV)

    mix = jnp.concatenate([o_a.reshape(B, L, GDN_V), o_r.reshape(B, L, RET_V)], axis=-1).astype(x.dtype)
    x = x + mix @ w_out

    h = _rmsnorm(x, ffn_norm_w)
    gate, up = jnp.split(h @ w_gate_up, 2, axis=-1)
    x = x + (jax.nn.silu(gate) * up) @ w_down
    return x, new_buf, s_gdn_new.astype(s_gdn.dtype), s_ret_new.astype(s_ret.dtype)


def setup_inputs(seed: int = 0) -> dict:
    key = jax.random.key(seed)
    ks = jax.random.split(key, 17)
    nrm = jax.random.normal
    dt = jnp.exp(jax.random.uniform(ks[9], (DEPTH, GDN_HEADS), minval=math.log(1e-3), maxval=math.log(1e-1)))
    return {
        'x_prompt': nrm(ks[0], (BATCH, SEQ, D_MODEL), jnp.float32),
        'x_sample': nrm(ks[1], (DEC_BATCH, DEC_SEQ, D_MODEL), jnp.float32),
        'state_gdn_conv': nrm(ks[2], (DEPTH, DEC_BATCH, CONV_W - 1, GDN_CONV_DIM), jnp.float32),
        'state_gdn': 0.05 * nrm(ks[3], (DEPTH, DEC_BATCH, GDN_HEADS, GDN_DK, GDN_DV), jnp.float32),
        'state_ret': 0.1 * nrm(ks[4], (DEPTH, DEC_BATCH, RET_HEADS, RET_DK, RET_DV), jnp.float32),
        'attn_norm_w': 1.0 + 0.01 * nrm(ks[5], (DEPTH, D_MODEL), jnp.float32),
        'w_in': nrm(ks[6], (DEPTH, D_MODEL, PROJ_WIDTH), jnp.float32) * D_MODEL ** -0.5,
        'conv_w': nrm(ks[7], (DEPTH, CONV_W, GDN_CONV_DIM), jnp.float32) * CONV_W ** -0.5,
        'a_log': jnp.log(jax.random.uniform(ks[8], (DEPTH, GDN_HEADS), minval=1.0, maxval=16.0)),
        'dt_bias': dt + jnp.log(-jnp.expm1(-dt)),
        'gdn_norm_w': 1.0 + 0.01 * nrm(ks[10], (DEPTH, GDN_DV), jnp.float32),
        'ret_gn_w': 1.0 + 0.01 * nrm(ks[11], (DEPTH, RET_V), jnp.float32),
        'w_out': nrm(ks[12], (DEPTH, MIX_WIDTH, D_MODEL), jnp.float32) * MIX_WIDTH ** -0.5,
        'ffn_norm_w': 1.0 + 0.01 * nrm(ks[13], (DEPTH, D_MODEL), jnp.float32),
        'w_gate_up': nrm(ks[14], (DEPTH, D_MODEL, 2 * D_FF), jnp.float32) * D_MODEL ** -0.5,
        'w_down': nrm(ks[15], (DEPTH, D_FF, D_MODEL), jnp.float32) * D_FF ** -0.5,
        'final_norm_w': 1.0 + 0.01 * nrm(ks[16], (D_MODEL,), jnp.float32),
    }


def reference(x_prompt, x_sample, state_gdn_conv, state_gdn, state_ret, attn_norm_w, w_in, conv_w,
              a_log, dt_bias, gdn_norm_w, ret_gn_w, w_out, ffn_norm_w, w_gate_up, w_down, final_norm_w):
    bp = x_prompt.shape[0]
    pos_p = jnp.arange(x_prompt.shape[1])
    pos_s = PAST_LEN + jnp.arange(x_sample.shape[1])
    hp, hs = x_prompt, x_sample
    pc, pg, pr, sc, sg, sr = [], [], [], [], [], []
    for l in range(DEPTH):
        lw = (attn_norm_w[l], w_in[l], conv_w[l], a_log[l], dt_bias[l], gdn_norm_w[l], ret_gn_w[l],
              w_out[l], ffn_norm_w[l], w_gate_up[l], w_down[l])
        conv0 = jnp.zeros((bp, CONV_W - 1, GDN_CONV_DIM), x_prompt.dtype)
        sg0 = jnp.zeros((bp, GDN_HEADS, GDN_DK, GDN_DV), state_gdn.dtype)
        sr0 = jnp.zeros((bp, RET_HEADS, RET_DK, RET_DV), state_ret.dtype)
        hp, c_p, g_p, r_p = _layer(hp, pos_p, conv0, sg0, sr0, *lw)
        hs, c_s, g_s, r_s = _layer(hs, pos_s, state_gdn_conv[l], state_gdn[l], state_ret[l], *lw)
        pc.append(c_p); pg.append(g_p); pr.append(r_p)
        sc.append(c_s); sg.append(g_s); sr.append(r_s)
    y_prompt = _rmsnorm(hp, final_norm_w)
    y_sample = _rmsnorm(hs, final_norm_w)
    return (y_prompt, y_sample, jnp.stack(pc), jnp.stack(pg), jnp.stack(pr),
            jnp.stack(sc), jnp.stack(sg), jnp.stack(sr))
```

```python
import math
from contextlib import ExitStack

import numpy as np
import ml_dtypes

import concourse.bass as bass
import concourse.mybir as mybir
from concourse.bass_utils import run_bass_kernel_spmd

F32 = mybir.dt.float32
BF16 = mybir.dt.bfloat16
AF = mybir.ActivationFunctionType
ALU = mybir.AluOpType
AX = mybir.AxisListType

D = 2048
KC = 16
PW = 8208
DFF = 5632
NH_G = 8
NH_R = 4
C = 128
TT = 256
NCH = TT // 128
PAST_LEN = 4096
DEC_SEQ = 16
RMS_EPS = 1e-6
GN_EPS = 1e-5
L2_EPS = 1e-6
HG = 4
NPART = 4
GPP = 11

OFF_Q, OFF_K, OFF_V, OFF_Z, OFF_BA, OFF_QR, OFF_KR, OFF_VR, OFF_GR = (
    0, 1024, 2048, 3072, 4096, 4112, 5136, 6160, 7184)


class _Op:
    __slots__ = ("eng", "fn", "dma", "id", "deps", "epoch", "sigval", "signal", "waits", "line")


def _caller_line():
    import sys
    f = sys._getframe(2)
    while f is not None and f.f_code.co_name in ('add', 'dma', 'mm_group', 'tr_group', 'act', 'tt', 'ts', 'stt', 'cp', 'red', 'memset'):
        f = f.f_back
    return f.f_lineno if f is not None else -1


class Sched:
    ENGS = ("pe", "act", "dve", "pool", "sp")

    def __init__(self):
        self.ops = {e: [] for e in self.ENGS}
        self.all = []
        self.lastw = {}
        self.readers = {}
        self.epoch = 0
        self.marks = []
        self.stream = None

    def add(self, eng, fn, reads=(), writes=(), dma=None):
        if self.stream is not None:
            self.stream.append((eng, fn, list(reads), list(writes), dma, _caller_line()))
            return None
        return self._add(eng, fn, reads, writes, dma, _caller_line())

    def begin_stream(self):
        self.stream = []

    def end_stream(self):
        st, self.stream = self.stream, None
        return st

    def merge(self, a, b, emit=True):
        na, nb = len(a), len(b)
        ia = ib = 0
        out = []
        while ia < na or ib < nb:
            if ib >= nb or (ia < na and ia * nb <= ib * na):
                out.append(a[ia]); ia += 1
            else:
                out.append(b[ib]); ib += 1
        if emit:
            for r in out:
                self._add(*r)
        return out

    def _add(self, eng, fn, reads, writes, dma, line):
        op = _Op()
        op.eng, op.fn, op.dma = eng, fn, dma
        op.id = len(self.all)
        op.epoch = self.epoch
        op.signal = False
        op.line = line
        deps = set()
        raw = set()
        ps_r = [r for r in reads if isinstance(r, tuple) and r[0] in ("pa", "pt")]
        if ps_r:
            reads = [r for r in reads if r not in ps_r]
            writes = list(writes) + ps_r
            for r in ps_r:
                w = self.lastw.get(r)
                if w is not None:
                    raw.add(w)
        for r in reads:
            w = self.lastw.get(r)
            if w is not None:
                deps.add(w)
                raw.add(w)
        for w_ in writes:
            w = self.lastw.get(w_)
            if w is not None:
                deps.add(w)
            for rid in self.readers.get(w_, ()):
                deps.add(rid)
        keep = []
        for d in deps:
            if d == op.id:
                continue
            dop = self.all[d]
            if dop.dma is None and dop.eng == eng:
                if eng == "pe":
                    continue
                if d not in raw:
                    continue
            keep.append(d)
        op.deps = keep
        for r in reads:
            lst = self.readers.setdefault(r, [])
            if dma is None:
                lst[:] = [x for x in lst if not (self.all[x].dma is None and self.all[x].eng == eng)]
            lst.append(op.id)
        for w_ in writes:
            self.lastw[w_] = op.id
            self.readers[w_] = []
        self.all.append(op)
        self.ops[eng].append(op)
        return op

    def mark(self, name):
        self.marks.append((name, len(self.all)))

    def truncate(self, n):
        self.all = self.all[:n]
        for e in self.ENGS:
            self.ops[e] = [o for o in self.ops[e] if o.id < n]
        op = _Op()
        op.eng, op.fn, op.dma = "sp", None, None
        op.id = len(self.all)
        op.epoch = self.epoch
        op.signal = False
        op.deps = [o.id for o in self.all if o.dma is not None]
        self.all.append(op)
        self.ops["sp"].append(op)

    def finalize(self):
        for op in self.all:
            for d in op.deps:
                self.all[d].signal = True
        cnt = {}
        dcnt = {}
        semkeys = []
        for op in self.all:
            if op.dma is not None:
                k = ("dma", op.dma)
                dcnt[k] = dcnt.get(k, 0) + 16
                op.sigval = (k, dcnt[k])
                op.signal = True
                if k not in semkeys:
                    semkeys.append(k)
            elif op.signal:
                k = ("eng", op.eng, op.epoch)
                cnt[k] = cnt.get(k, 0) + 1
                op.sigval = (k, cnt[k])
                if k not in semkeys:
                    semkeys.append(k)
        for e in self.ENGS:
            seen = {}
            for op in self.ops[e]:
                waits = []
                for d in op.deps:
                    k, v = self.all[d].sigval
                    if seen.get(k, 0) >= v:
                        continue
                    seen[k] = v
                    waits.append((k, v))
                best = {}
                for k, v in waits:
                    best[k] = max(best.get(k, 0), v)
                op.waits = list(best.items())
        return semkeys


def build_program(n_pre, n_own, debug=None, trunc=None):
    nc = bass.Bass("TRN2", target_bir_lowering=False)
    es = ExitStack()
    S = Sched()

    def dram(name, shape, dt, kind):
        return nc.dram_tensor(name, list(shape), dt, kind=kind).ap()

    NTOK = 128 + (n_pre + n_own) * TT
    xs_d = dram("xs", [128, D], F32, "ExternalInput")
    xp_d = dram("xp", [max(n_pre, 1) * TT, D], F32, "ExternalInput")
    xo_d = dram("xo", [n_own * TT, D], F32, "ExternalInput")
    cos_d = dram("cosd", [NTOK, 128], F32, "ExternalInput")
    sin_d = dram("sind", [NTOK, 128], F32, "ExternalInput")
    sg0_d = dram("sg0", [128, NH_G, 128], F32, "ExternalInput")
    sr0_d = dram("sr0", [128, NH_R, 2, 256], F32, "ExternalInput")
    hist0_d = dram("hist0", [128, 3, 24], F32, "ExternalInput")
    w_in_d = dram("w_in", [D, PW], F32, "ExternalInput")
    w_out_d = dram("w_out", [D, D], F32, "ExternalInput")
    w_gu_d = dram("w_gu", [D, 2 * DFF], F32, "ExternalInput")
    w_dn_d = dram("w_dn", [DFF, D], F32, "ExternalInput")
    anw_d = dram("anw", [128, KC], F32, "ExternalInput")
    fnw_d = dram("fnw", [128, KC], F32, "ExternalInput")
    finw_d = dram("finw", [128, D], F32, "ExternalInput")
    convw_d = dram("convw", [128, 24, 4], F32, "ExternalInput")
    alog_d = dram("alog", [128, NH_G], F32, "ExternalInput")
    dtb_d = dram("dtb", [128, NH_G], F32, "ExternalInput")
    gnw_d = dram("gnw", [128, 128], F32, "ExternalInput")
    rgw_d = dram("rgw", [128, 1024], F32, "ExternalInput")
    cf_d = dram("cf32", [128, CF_COLS], F32, "ExternalInput")
    cb_d = dram("cbf16", [128, CB_COLS], BF16, "ExternalInput")
    ys_d = dram("y_s", [DEC_SEQ, D], F32, "ExternalOutput")
    yo_d = dram("y_o", [n_own * TT, D], F32, "ExternalOutput")
    ocs_d = dram("oc_s", [3, 3072], F32, "ExternalOutput")
    ogs_d = dram("og_s", [NH_G, 128, 128], F32, "ExternalOutput")
    ors_d = dram("or_s", [NH_R, 256, 256], F32, "ExternalOutput")
    ocp_d = dram("oc_p", [3, 3072], F32, "ExternalOutput")
    ogp_d = dram("og_p", [NH_G, 128, 128], F32, "ExternalOutput")
    orp_d = dram("or_p", [NH_R, 256, 256], F32, "ExternalOutput")
    wi_b = dram("wi_b", [D, PW], BF16, "Internal")
    wo_b = dram("wo_b", [D, D], BF16, "Internal")
    wg_b = dram("wg_b", [D, 2 * DFF], BF16, "Internal")
    wd_b = dram("wd_b", [DFF, D], BF16, "Internal")
    wi_v = wi_b.rearrange("(kc p) n -> p kc n", p=128)
    wo_v = wo_b.rearrange("(kc p) n -> p kc n", p=128)
    wg_v = wg_b.rearrange("(kc p) n -> p kc n", p=128)
    wd_v = wd_b.rearrange("(kc p) n -> p kc n", p=128)

    sb_bytes = [0]

    def sb(name, shape, dt):
        n = 1
        for x in shape[1:]:
            n *= x
        sb_bytes[0] += n * (4 if dt == F32 else 2)
        return es.enter_context(nc.sbuf_tensor(name, list(shape), dt))

    def ps(name, shape, dt):
        return es.enter_context(nc.psum_tensor(name, list(shape), dt))

    xt = sb("xt", [128, NCH, D], F32)
    act16 = sb("act16", [128, KC, TT], BF16)
    mixT = sb("mixT", [128, KC, TT], BF16)
    NSLOT = 3
    ring = [sb(f"ring{i}", [128, KC, 512], BF16) for i in range(NSLOT)]
    wba = sb("wba", [128, KC, 16], BF16)
    Sg = sb("Sg", [128, NH_G, 128], F32)
    Sgb = sb("Sgb", [128, NH_G, 128], BF16)
    Sr = sb("Sr", [128, NH_R, 2, 256], F32)
    Srb = sb("Srb", [128, NH_R, 2, 256], BF16)
    hist = sb("hist", [128, 3, 24], F32)
    cf = sb("cf", [128, CF_COLS], F32)
    cb = sb("cb", [128, CB_COLS], BF16)
    anw = sb("anw_s", [128, KC], F32)
    fnw = sb("fnw_s", [128, KC], F32)
    convw = sb("convw_s", [128, 24, 4], F32)
    alog = sb("alog_s", [128, NH_G], F32)
    dtb = sb("dtb_s", [128, NH_G], F32)
    nega = sb("nega", [128, NH_G], F32)
    gnw = sb("gnw_s", [128, 128], F32)
    rgw = sb("rgw_s", [128, 1024], F32)
    cosT = sb("cosT", [128, NCH, 128], F32)
    sinT = sb("sinT", [128, NCH, 128], F32)
    junk = sb("junk", [128, 512], BF16)
    ss4 = sb("ss4", [128, 16], F32)
    ssum = sb("ssum", [128, 4], F32)
    rstd = sb("rstd", [128, 4], F32)
    hb = [sb(f"hb{i}", [128, D], BF16) for i in range(1)]
    ba = sb("ba", [128, NCH, 16], F32)
    rowsrc = sb("rowsrc", [128, NCH, 16], F32)
    tm8 = [sb(f"tm8_{i}", [128, NCH, 8], F32) for i in range(3)]
    gtm = sb("gtm", [128, NCH, 8], F32)
    gtot = sb("gtot", [128, NCH, 8], F32)
    bge = sb("bge", [128, NCH, 8], F32)
    dks = sb("dks", [128, NCH, 8], F32)
    glb = sb("glb", [128, NCH, 8], F32)
    pre = [sb(f"pre{i}", [128, TT + 3], F32) for i in range(2)]
    cacc = [sb(f"cacc{i}", [128, TT], F32) for i in range(2)]
    qs = [sb(f"qs{i}", [128, TT], F32) for i in range(2)]
    sqb = [sb(f"sqb{i}", [128, TT], BF16) for i in range(2)]
    rsb = [sb(f"rsb{i}", [128, TT], F32) for i in range(2)]
    kTn = sb("kTn", [128, NH_G, TT], BF16)
    qTn = sb("qTn", [128, NH_G, TT], BF16)
    vT = sb("vT", [128, NH_G, TT], BF16)
    zg = sb("zg", [128, NCH, 1024], BF16)
    f512 = [sb(f"f512_{i}", [128, 512], F32) for i in range(6)]
    Dtig = [sb(f"Dti{i}", [128, HG, 128], F32) for i in range(2)]
    Dtsg = [sb(f"Dts{i}", [128, HG, 128], F32) for i in range(2)]
    PHYS = {"AT": "P1", "kb": "P0", "vb": "PT1", "kbg": "IpP", "kdec": "P0", "kcdT": "P1", "u": "PT0", "mixs": "A"}
    b512g = [{n: sb("b%d_%s" % (i, n), [128, HG, 128], BF16) for n in
              ("A", "P0", "P1", "PT0", "PT1", "IpP", "RT0", "RT1", "qd", "attnT")} for i in range(2)]
    b512 = {"mixr": sb("b_mixr", [128, HG, 128], BF16)}
    krT = sb("krT", [128, 4, TT], BF16)
    qrT = sb("qrT", [128, 4, TT], BF16)
    qdT = sb("qdT", [128, 4, TT], BF16)
    kdr = sb("kdr", [128, NCH, 512], BF16)
    vr = sb("vr", [128, NCH, 512], BF16)
    grw = sb("grw", [128, NCH, 512], BF16)
    rot = sb("rot", [128, 512], BF16)
    rot1 = sb("rot1", [128, 512], BF16)
    attr = sb("attr", [128, 2, 128], BF16)
    st8 = sb("st8", [128, 16], F32)
    actT = sb("actT", [128, GPP, TT], BF16)
    h72 = sb("h72", [72, 128], F32)
    pa = [ps(f"pa{i}", [128, 512], F32) for i in range(6)]
    pt = [ps(f"pt{i}", [128, 1024], BF16) for i in range(2)]
    rr = {}
    pool_sel = ["A"]
    POOLS = {
        "pa": {"A": list(range(6)), "G0": [0, 1, 2], "G1": [3, 4, 5], "H0": [0, 1], "H1": [2, 3], "Z": [4, 5]},
        "pt": {"A": [0, 1], "G0": [0], "G1": [1], "H0": [0], "H1": [1], "Z": [0]},
        "f512": {"A": list(range(6)), "G0": [0, 1, 2], "G1": [3, 4, 5], "H0": [0, 1, 2], "H1": [3, 4, 5], "Z": [0]},
        "ring": {"A": [0, 1, 2], "H0": [0, 1], "H1": [2]},
    }

    def _rr(kind):
        sel = pool_sel[0] if pool_sel[0] in POOLS[kind] else "A"
        lst = POOLS[kind][sel]
        k = (kind, sel)
        i = rr.get(k, 0)
        rr[k] = (i + 1) % len(lst)
        return lst[i]

    def get_pa():
        i = _rr("pa")
        return pa[i], ("pa", i)

    def get_pt():
        i = _rr("pt")
        return pt[i], ("pt", i)

    def get_f():
        i = _rr("f512")
        return f512[i], ("f512", i)

    def get_ring():
        i = _rr("ring")
        return ring[i], ("ring", i)

    o = 0
    def cfv(n):
        nonlocal o
        v = cf[:, o:o + n]
        o += n
        return v
    mask_incl = cfv(128)
    mask_strict = cfv(128)
    ident_f = cfv(128)
    ones_f = cfv(128)
    DrT = cfv(512).rearrange("p (h i) -> p h i", h=4)
    EBr = cfv(512).rearrange("p (h i) -> p h i", h=4)
    _kf = cfv(4)
    _ks = cfv(4)
    vm_samp = cfv(1)
    i16 = cf[0:16, o:o + 16]; o += 16
    ones16 = cf[0:16, o:o + 128]; o += 128
    assert o == CF_COLS
    ident_b = cb[:, 0:128]
    ones_b = cb[:, 128:256]
    bd8 = cb[:, 256:384]
    kdc_full = cb[:, 896:900]
    kdc_samp = cb[:, 900:904]
    mX = [cb[:, 384 + i * 128:512 + i * 128] for i in range(4)]

    def bc_mid(ap2, n):
        return ap2.unsqueeze(1).to_broadcast([ap2.shape[0], n, ap2.shape[1]])

    def bc_last(ap2, n):
        return ap2.unsqueeze(2).to_broadcast([ap2.shape[0], ap2.shape[1], n])

    def v3(ap2, a):
        return ap2.rearrange("p (a b) -> p a b", a=a)

    def dma(q, out, in_, reads, writes, key):
        S.add(q, lambda e: e.dma_start(out=out, in_=in_), reads=reads, writes=writes, dma=key)

    def mm_group(items, reads, writes):
        def fn(e):
            last = None
            for (o_, l_, r_, st, sp) in items:
                last = e.matmul(o_, lhsT=l_, rhs=r_, start=st, stop=sp)
            return last
        S.add("pe", fn, reads=reads, writes=writes)

    def tr_group(items, reads, writes):
        def fn(e):
            last = None
            for (o_, i_, id_) in items:
                last = e.transpose(out=o_, in_=i_, identity=id_)
            return last
        S.add("pe", fn, reads=reads, writes=writes)

    def act(out, in_, func, reads, writes, bias=None, scale=None, accum_out=None):
        kw = {}
        if bias is not None:
            kw["bias"] = bias
        if scale is not None:
            kw["scale"] = scale
        if accum_out is not None:
            kw["accum_out"] = accum_out
        S.add("act", lambda e: e.activation(out=out, in_=in_, func=func, **kw), reads=reads, writes=writes)

    def tt(eng, out, in0, in1, op, reads, writes):
        S.add(eng, lambda e: e.tensor_tensor(out=out, in0=in0, in1=in1, op=op), reads=reads, writes=writes)

    def ts(eng, out, in0, s1, s2, op0, op1, reads, writes):
        if s2 is None:
            S.add(eng, lambda e: e.tensor_scalar(out=out, in0=in0, scalar1=s1, scalar2=None, op0=op0),
                  reads=reads, writes=writes)
        else:
            S.add(eng, lambda e: e.tensor_scalar(out=out, in0=in0, scalar1=s1, scalar2=s2, op0=op0, op1=op1),
                  reads=reads, writes=writes)

    def stt(eng, out, in0, scalar, in1, op0, op1, reads, writes):
        S.add(eng, lambda e: e.scalar_tensor_tensor(out=out, in0=in0, scalar=scalar, in1=in1, op0=op0, op1=op1),
              reads=reads, writes=writes)

    def cp(eng, out, in_, reads, writes):
        if eng == "act":
            S.add("act", lambda e: e.copy(out=out, in_=in_), reads=reads, writes=writes)
        else:
            S.add(eng, lambda e: e.tensor_copy(out=out, in_=in_), reads=reads, writes=writes)

    def red(eng, out, in_, reads, writes):
        S.add(eng, lambda e: e.tensor_reduce(out=out, in_=in_, axis=AX.X, op=ALU.add), reads=reads, writes=writes)

    def memset(eng, ap, val, writes):
        S.add(eng, lambda e: e.memset(ap, val), reads=(), writes=writes)

    wres = {"wi": [], "wo": [], "wg": [], "wd": []}

    def cast(name, dst, src, nparts):
        rows = src.shape[0] // nparts
        for i in range(nparts):
            r = (name, i)
            wres[name].append(r)
            dma("pool", dst[i * rows:(i + 1) * rows, :], src[i * rows:(i + 1) * rows, :], (), [r], r)

    cast("wi", wi_b, w_in_d, 4)
    for nm, dst, src in (("cf", cf, cf_d), ("cb", cb, cb_d), ("anw", anw, anw_d), ("fnw", fnw, fnw_d),
                         ("convw", convw, convw_d), ("alog", alog, alog_d),
                         ("dtb", dtb, dtb_d), ("gnw", gnw, gnw_d), ("rgw", rgw, rgw_d),
                         ("Sg", Sg, sg0_d), ("Sr", Sr, sr0_d), ("hist", hist, hist0_d)):
        dma("sp", dst[:], src[:], (), [nm], nm)
    cast("wo", wo_b, w_out_d, 2)
    cast("wg", wg_b, w_gu_d, 4)
    cast("wd", wd_b, w_dn_d, 4)
    dma("sp", wba[:], wi_v[:, :, OFF_BA:OFF_BA + 16], wres["wi"], ["wba"], "wba")
    act(nega[:], alog[:], AF.Exp, ["alog"], ["nega"])
    ts("dve", nega[:], nega[:], -1.0, None, ALU.mult, None, ["nega"], ["nega"])
    cp("act", Sgb[:], Sg[:], ["Sg"], ["Sgb"])
    cp("act", Srb[:], Sr[:], ["Sr"], ["Srb"])

    def rms_to_act16(nch, wcol, wname, first_load=None):
        memset("dve", ss4[:], 0.0, ["ss4"])
        for c in range(nch):
            xr = ("xt", c)
            for q in range(4):
                act(junk[:], xt[:, c, q * 512:(q + 1) * 512], AF.Square, [xr, "ss4"], ["junk", ("ss4", c, q)],
                    accum_out=ss4[:, c * 4 + q:c * 4 + q + 1])
            red("dve", ssum[:, c:c + 1], ss4[:, c * 4:c * 4 + 4], [("ss4", c, q) for q in range(4)], [("ssum", c)])
            act(rstd[:, c:c + 1], ssum[:, c:c + 1], AF.Ln, [("ssum", c)], [("rstd", c)], bias=RMS_EPS, scale=1.0 / D)
            act(rstd[:, c:c + 1], rstd[:, c:c + 1], AF.Exp, [("rstd", c)], [("rstd", c)], scale=-0.5)
            hbt = hb[0]
            hr = ("hb", 0)
            act(hbt[:], xt[:, c, :], AF.Copy, [xr, ("rstd", c)], [hr], scale=rstd[:, c:c + 1])
            for half in range(2):
                ptt, ptr = get_pt()
                tr_group([(ptt[:, i * 128:(i + 1) * 128], hbt[:, (half * 8 + i) * 128:(half * 8 + i + 1) * 128], ident_b)
                          for i in range(8)], [hr, "cb"], [ptr])
                tt("dve", act16[:, half * 8:(half + 1) * 8, c * 128:(c + 1) * 128], v3(ptt[:], 8),
                   bc_last(wcol[:, half * 8:(half + 1) * 8], 128), ALU.mult, [ptr, wname], [("hT", c, half)])

    def hT_res(nch):
        return [("hT", c, h) for c in range(nch) for h in range(2)]

    def load_w(view, c0, cw, k0, nk, srcres):
        slot, sres = get_ring()
        dma("sp", slot[:, 0:nk, 0:cw], view[:, k0:k0 + nk, c0:c0 + cw], srcres, [sres], sres)
        return slot, sres

    td_i = [0]

    def emit_tile(td):
        nch = td["nch"]
        T = nch * 128
        full = td["full"]
        L = td["L"]
        samp = td["samp"]
        x_d = td["x"]
        tok0 = td["tok0"]
        need_q = full or td["qhist"]
        kdc = kdc_samp if samp else kdc_full
        gl_r = td["gl_r"]
        td_i[0] += 1
        if td_i[0] % 4 == 1:
            S.epoch += 1
        S.mark('tile%d_A' % td_i[0])
        for c in range(nch):
            dma("sp", xt[:, c, :], x_d[c * 128:(c + 1) * 128, :], (), [("xt", c)], ("xt", c))
        dma("sp", cosT[:, 0:nch, :], cos_d[tok0:tok0 + T, :].rearrange("(c p) f -> p c f", p=128), (), ["cos"], "cos")
        dma("sp", sinT[:, 0:nch, :], sin_d[tok0:tok0 + T, :].rearrange("(c p) f -> p c f", p=128), (), ["sin"], "sin")
        rms_to_act16(nch, anw, "anw")
        HT = hT_res(nch)

        S.mark('tile%d_B0' % td_i[0])
        pool_sel[0] = "Z"
        S.begin_stream()
        pba, pbar = get_pa()
        for c in range(nch):
            mm_group([(pba[:, c * 16:(c + 1) * 16], act16[:, kc, c * 128:(c + 1) * 128], wba[:, kc, :], kc == 0, kc == KC - 1)
                      for kc in range(KC)], [("hT", c, 0), ("hT", c, 1), "wba"], [pbar] if c == nch - 1 else [pbar])
        bav = ba[:, 0:nch, :]
        cp("dve", bav, v3(pba[:, 0:nch * 16], nch), [pbar], ["ba"])
        e8, b8, x8 = tm8[0][:, 0:nch, :], rowsrc[:, 0:nch, 8:16], tm8[1][:, 0:nch, :]
        act(e8, ba[:, 0:nch, 0:8], AF.Exp, ["ba"], ["e8"], scale=-1.0)
        ts("dve", e8, e8, 1.0, None, ALU.add, None, ["e8"], ["e8"])
        S.add("dve", lambda e: e.reciprocal(out=b8, in_=e8), reads=["e8"], writes=["beta"])
        tt("dve", x8, ba[:, 0:nch, 8:16], bc_mid(dtb[:], nch), ALU.add, ["ba", "dtb"], ["x8"])
        act(x8, x8, AF.Exp, ["x8"], ["x8"])
        act(x8, x8, AF.Ln, ["x8"], ["x8"], bias=1.0, scale=1.0)
        gv = gtm[:, 0:nch, :]
        tt("dve", gv, x8, bc_mid(nega[:], nch), ALU.mult, ["x8", "nega"], ["g"])
        if samp:
            ts("dve", gv, gv, vm_samp[:, 0:1], None, ALU.mult, None, ["g", "cf"], ["g"])
            ts("dve", b8, b8, vm_samp[:, 0:1], None, ALU.mult, None, ["beta", "cf"], ["beta"])
        pgc, pgcr = get_pa()
        mm_group([(pgc[:, c * 8:(c + 1) * 8], mask_incl, gtm[:, c, :], True, True) for c in range(nch)] +
                 [(pgc[:, 32 + c * 8:32 + (c + 1) * 8], ones_f, gtm[:, c, :], True, True) for c in range(nch)],
                 ["g", "cf"], [pgcr])
        gcv = rowsrc[:, 0:nch, 0:8]
        cp("dve", gcv, v3(pgc[:, 0:nch * 8], nch), [pgcr], ["gc"])
        gtv = gtot[:, 0:nch, :]
        cp("dve", gtv, v3(pgc[:, 32:32 + nch * 8], nch), [pgcr], ["gtot"])
        eg = tm8[2][:, 0:nch, :]
        act(eg, gcv, AF.Exp, ["gc"], ["eg"])
        tt("dve", bge[:, 0:nch, :], b8, eg, ALU.mult, ["beta", "eg"], ["bge"])
        tt("dve", dks[:, 0:nch, :], gtv, gcv, ALU.subtract, ["gtot", "gc"], ["dks"])
        act(dks[:, 0:nch, :], dks[:, 0:nch, :], AF.Exp, ["dks"], ["dks"])
        act(glb[:, 0:nch, :], gtv, AF.Exp, ["gtot"], ["glb"])

        b0_stream = S.end_stream()
        pool_sel[0] = "A"
        S.mark('tile%d_B' % td_i[0])
        def conv_group(which, hh, h, pacc, paccr, i2):
            g = which * 8 + h
            pr = pre[i2]
            prr = ("pre", i2)
            cp("pool", pr[:, 0:3], hist[:, :, g], ["hist", ("hist", g)], [prr])
            cp("act", pr[:, 3:3 + T], pacc[:, 0:T], [paccr], [(prr, "b")])
            cp("pool", hist[:, :, g], pr[:, L:L + 3], [prr, (prr, "b")], [("hist", g)])
            return pr, [prr, (prr, "b")]


        def Kc(gi, name):
            return ("c%d" % gi, PHYS.get(name, name))

        def Bc(gi, name):
            return b512g[gi][PHYS.get(name, name)]

        def proj_head(which, hh, h, slot, sres):
            pacc, paccr = get_pa()
            mm_group([(pacc[:, 0:T], slot[:, kc, hh * 128:(hh + 1) * 128], act16[:, kc, 0:T], kc == 0, kc == KC - 1)
                      for kc in range(KC)], HT + [sres], [paccr])
            i2 = hh % 2
            pr, prres = conv_group(which, hh, h, pacc, paccr, i2)
            if which == 0 and not full:
                return
            g = which * 8 + h
            ca = cacc[i2]
            car = ("cacc", i2)
            ts("dve", ca[:, 0:T], pr[:, 0:T], convw[:, g, 0:1], None, ALU.mult, None, prres + ["convw"], [car])
            for j in range(1, 4):
                stt("dve", ca[:, 0:T], pr[:, j:j + T], convw[:, g, j:j + 1], ca[:, 0:T], ALU.mult, ALU.add,
                    prres + [car, "convw"], [car])
            if which == 2:
                act(vT[:, h, 0:T], ca[:, 0:T], AF.Silu, [car], [("vT", h)])
                return
            q_ = qs[i2]
            qr_ = ("qs", i2)
            act(q_[:, 0:T], ca[:, 0:T], AF.Silu, [car], [qr_])
            act(sqb[i2][:, 0:T], q_[:, 0:T], AF.Square, [qr_], [("sqb", i2)])
            pss, pssr = get_pa()
            mm_group([(pss[:, 0:T], ones_b, sqb[i2][:, 0:T], True, True)], [("sqb", i2), "cb"], [pssr])
            act(rsb[i2][:, 0:T], pss[:, 0:T], AF.Ln, [pssr], [("rsb", i2)], bias=L2_EPS, scale=1.0)
            act(rsb[i2][:, 0:T], rsb[i2][:, 0:T], AF.Exp, [("rsb", i2)], [("rsb", i2)], scale=-0.5)
            if which == 1:
                tt("dve", kTn[:, h, 0:T], q_[:, 0:T], rsb[i2][:, 0:T], ALU.mult, [qr_, ("rsb", i2)], [("kTn", h)])
            else:
                stt("dve", qTn[:, h, 0:T], q_[:, 0:T], 128.0 ** -0.5, rsb[i2][:, 0:T], ALU.mult, ALU.mult,
                    [qr_, ("rsb", i2)], [("qTn", h)])

        b1_list = []
        for gi in range(NH_G // HG):
            h0 = gi * HG
            for which, off, dst in ((1, OFF_K, kTn), (2, OFF_V, vT), (0, OFF_Q, qTn)):
                if which == 0 and not need_q:
                    continue
                S.begin_stream()
                slot, sres = load_w(wi_v, off + h0 * 128, HG * 128, 0, KC, wres["wi"])
                b1_list.extend(S.end_stream())
                hstreams = []
                for hh in range(HG):
                    h = h0 + hh
                    pool_sel[0] = "H%d" % (hh % 2)
                    S.begin_stream()
                    proj_head(which, hh, h, slot, sres)
                    hstreams.append(S.end_stream())
                    pool_sel[0] = "A"
                    if hh % 2 == 1:
                        b1_list.extend(S.merge(hstreams[hh - 1], hstreams[hh], emit=False))
            if full:
                pool_sel[0] = "H0"
                S.begin_stream()
                slot, sres = load_w(wi_v, OFF_Z + h0 * 128, HG * 128, 0, KC, wres["wi"])
                for c in range(nch):
                    pz, pzr = get_pa()
                    mm_group([(pz[:, :], act16[:, kc, c * 128:(c + 1) * 128], slot[:, kc, :], kc == 0, kc == KC - 1)
                              for kc in range(KC)], [("hT", c, 0), ("hT", c, 1), sres], [pzr])
                    ft, ftr = get_f()
                    act(ft[:], pz[:], AF.Silu, [pzr], [ftr])
                    tt("pool", v3(zg[:, c, h0 * 128:(h0 + HG) * 128], HG), v3(ft[:], HG), bc_mid(gnw[:], HG), ALU.mult,
                       [ftr, "gnw"], [("zg", c, gi)])
                b1_list.extend(S.end_stream())
                pool_sel[0] = "A"
        S.merge(b1_list, b0_stream)

        def gdn_chain(gi, c):
            h0 = gi * HG
            cs = slice(c * 128, (c + 1) * 128)
            hs = slice(h0, h0 + HG)
            K_ = lambda n: Kc(gi, n)
            B_ = lambda n: Bc(gi, n)
            KT = [("kTn", h0 + hh) for hh in range(HG)]
            QT = [("qTn", h0 + hh) for hh in range(HG)]
            VT = [("vT", h0 + hh) for hh in range(HG)]
            Dts_, Dti_ = Dtsg[gi], Dtig[gi]
            dg, dgr = get_f()
            tt("dve", v3(dg[:], HG), bc_mid(ident_f, HG), bc_last(rowsrc[:, c, h0:h0 + HG], 128), ALU.mult, ["gc", "cf"], [dgr])
            db, dbr = get_f()
            tt("dve", v3(db[:], HG), bc_mid(ident_f, HG), bc_last(rowsrc[:, c, 8 + h0:8 + h0 + HG], 128), ALU.mult,
               ["beta", "cf"], [dbr])
            pgb, pgbr = get_pa()
            pbb, pbbr = get_pa()
            mm_group([(pgb[:, r * 128:(r + 1) * 128], ones_f, dg[:, r * 128:(r + 1) * 128], True, True) for r in range(HG)],
                     [dgr, "cf"], [pgbr])
            mm_group([(pbb[:, r * 128:(r + 1) * 128], ones_f, db[:, r * 128:(r + 1) * 128], True, True) for r in range(HG)],
                     [dbr, "cf"], [pbbr])
            f0, f0r = get_f()
            tt("dve", v3(f0[:], HG), v3(pgb[:], HG), bc_last(rowsrc[:, c, h0:h0 + HG], 128), ALU.subtract, [pgbr, "gc"], [f0r])
            ts("dve", f0[:], f0[:], 0.0, None, ALU.min, None, [f0r], [f0r])
            act(f0[:], f0[:], AF.Exp, [f0r], [f0r])
            tt("dve", Dts_[:], v3(f0[:], HG), bc_mid(mask_strict, HG), ALU.mult, [f0r, "cf"], [K_("Dts")])
            if full:
                tt("pool", Dti_[:], v3(f0[:], HG), bc_mid(mask_incl, HG), ALU.mult, [f0r, "cf"], [K_("Dti")])
            tt("dve", B_("kb")[:], kTn[:, hs, cs], v3(pbb[:], HG), ALU.mult, KT + [pbbr], [K_("kb")])
            if full:
                act(B_("qd")[:], v3(pgb[:], HG), AF.Exp, [pgbr], [K_("qd")])
                tt("dve", B_("qd")[:], qTn[:, hs, cs], B_("qd")[:], ALU.mult, QT + [K_("qd")], [K_("qd")])
            pat, patr = get_pa()
            mm_group([(pat[:, hh * 128:(hh + 1) * 128], kTn[:, h0 + hh, cs], B_("kb")[:, hh, :], True, True) for hh in range(HG)],
                     KT + [K_("kb")], [patr])
            tt("dve", B_("AT")[:], v3(pat[:], HG), Dts_[:], ALU.mult, [patr, K_("Dts")], [K_("AT")])
            if full:
                pqk, pqkr = get_pa()
                mm_group([(pqk[:, hh * 128:(hh + 1) * 128], kTn[:, h0 + hh, cs], qTn[:, h0 + hh, cs], True, True) for hh in range(HG)],
                         KT + QT, [pqkr])
                tt("dve", B_("attnT")[:], v3(pqk[:], HG), Dti_[:], ALU.mult, [pqkr, K_("Dti")], [K_("attnT")])
            ptt, ptr = get_pt()
            tr_group([(ptt[:, hh * 128:(hh + 1) * 128], B_("AT")[:, hh, :], ident_b) for hh in range(HG)], [K_("AT"), "cb"], [ptr])
            cp("act", B_("A")[:], v3(ptt[:, 0:HG * 128], HG), [ptr], [K_("A")])
            tt("dve", B_("PT0")[:], B_("AT")[:], bc_mid(bd8, HG), ALU.mult, [K_("AT"), "cb"], [K_("PT0")])
            tt("pool", B_("P0")[:], B_("A")[:], bc_mid(bd8, HG), ALU.mult, [K_("A"), "cb"], [K_("P0")])
            tt("dve", B_("RT0")[:], bc_mid(ident_b, HG), B_("PT0")[:], ALU.subtract, [K_("PT0"), "cb"], [K_("RT0")])

            def mm4(lname, rname):
                p_, pr_ = get_pa()
                mm_group([(p_[:, hh * 128:(hh + 1) * 128], B_(lname)[:, hh, :], B_(rname)[:, hh, :], True, True)
                          for hh in range(HG)], [K_(lname), K_(rname)], [pr_])
                return p_, pr_
            pp, ppr = mm4("PT0", "P0")
            ppt, pptr = mm4("P0", "PT0")
            tt("dve", B_("IpP")[:], v3(pp[:], HG), bc_mid(ident_b, HG), ALU.add, [ppr, "cb"], [K_("IpP")])
            cp("act", B_("P1")[:], v3(pp[:], HG), [ppr], [K_("P1")])
            cp("act", B_("PT1")[:], v3(ppt[:], HG), [pptr], [K_("PT1")])
            prt_, prtr_ = mm4("IpP", "RT0")
            cp("act", B_("RT1")[:], v3(prt_[:], HG), [prtr_], [K_("RT1")])
            pp, ppr = mm4("PT1", "P1")
            tt("dve", B_("IpP")[:], v3(pp[:], HG), bc_mid(ident_b, HG), ALU.add, [ppr, "cb"], [K_("IpP")])
            prt_, prtr_ = mm4("IpP", "RT1")
            cp("act", B_("RT0")[:], v3(prt_[:], HG), [prtr_], [K_("RT0")])
            RTn = "RT0"
            for li in range(4):
                RTc = "RT1" if RTn == "RT0" else "RT0"
                tt("dve", B_("P0")[:], B_("A")[:], bc_mid(mX[li], HG), ALU.mult, [K_("A"), "cb"], [K_("P0")])
                ptt, ptr = get_pt()
                tr_group([(ptt[:, hh * 128:(hh + 1) * 128], B_(RTn)[:, hh, :], ident_b) for hh in range(HG)],
                         [K_(RTn), "cb"], [ptr])
                cp("act", B_("PT0")[:], v3(ptt[:, 0:HG * 128], HG), [ptr], [K_("PT0")])
                py_, pyr_ = mm4("P0", RTn)
                cp("act", B_("P1")[:], v3(py_[:], HG), [pyr_], [K_("P1")])
                pz_, pzr_ = mm4("PT0", "P1")
                tt("dve", B_(RTc)[:], B_(RTn)[:], v3(pz_[:], HG), ALU.subtract, [K_(RTn), pzr_], [K_(RTc)])
                RTn = RTc
            RT = B_(RTn)
            ptt, ptr = get_pt()
            tr_group([(ptt[:, hh * 128:(hh + 1) * 128], vT[:, h0 + hh, cs], ident_b) for hh in range(HG)], VT + ["cb"], [ptr])
            tt("dve", B_("vb")[:], v3(ptt[:, 0:HG * 128], HG), bc_last(rowsrc[:, c, 8 + h0:8 + h0 + HG], 128), ALU.mult,
               [ptr, "beta"], [K_("vb")])
            ptt, ptr = get_pt()
            tr_group([(ptt[:, hh * 128:(hh + 1) * 128], kTn[:, h0 + hh, cs], ident_b) for hh in range(HG)], KT + ["cb"], [ptr])
            tt("dve", B_("kbg")[:], v3(ptt[:, 0:HG * 128], HG), bc_last(bge[:, c, h0:h0 + HG], 128), ALU.mult,
               [ptr, "bge"], [K_("kbg")])
            tt("dve", B_("kdec")[:], v3(ptt[:, 0:HG * 128], HG), bc_last(dks[:, c, h0:h0 + HG], 128), ALU.mult,
               [ptr, "dks"], [K_("kdec")])
            pu0, pu0r = mm4(RTn, "vb")
            u0s, u0sr = get_f()
            cp("act", u0s[:], pu0[:], [pu0r], [u0sr])
            pkc, pkcr = mm4("kbg", RTn)
            cp("act", B_("kcdT")[:], v3(pkc[:], HG), [pkcr], [K_("kcdT")])
            SG = [("Sgb", h0 + hh) for hh in range(HG)]
            pw, pwr = get_pa()
            mm_group([(pw[:, hh * 128:(hh + 1) * 128], B_("kcdT")[:, hh, :], Sgb[:, h0 + hh, :], True, True) for hh in range(HG)],
                     [K_("kcdT"), "Sgb"] + SG, [pwr])
            tt("dve", B_("u")[:], v3(u0s[:], HG), v3(pw[:], HG), ALU.subtract, [u0sr, pwr], [K_("u")])
            if full:
                po, por = get_pa()
                items = []
                for hh in range(HG):
                    items.append((po[:, hh * 128:(hh + 1) * 128], B_("qd")[:, hh, :], Sgb[:, h0 + hh, :], True, False))
                    items.append((po[:, hh * 128:(hh + 1) * 128], B_("attnT")[:, hh, :], B_("u")[:, hh, :], False, True))
                mm_group(items, [K_("qd"), K_("attnT"), K_("u"), "Sgb"] + SG, [por])
                osb, osbr = get_f()
                cp("act", osb[:], po[:], [por], [osbr])
            pds, pdsr = mm4("kdec", "u")
            for hh in range(HG):
                h = h0 + hh
                stt("dve", Sg[:, h, :], Sg[:, h, :], glb[:, c, h:h + 1], pds[:, hh * 128:(hh + 1) * 128], ALU.mult, ALU.add,
                    ["Sg", ("Sg", h), "glb", pdsr], [("Sg", h)])
            cp("act", Sgb[:, h0:h0 + HG, :], Sg[:, h0:h0 + HG, :], ["Sg"] + [("Sg", h0 + hh) for hh in range(HG)], SG)
            if full:
                sq_, sqr_ = get_f()
                tt("pool", sq_[:], osb[:], osb[:], ALU.mult, [osbr], [sqr_])
                red("dve", st8[:, gi * 4:gi * 4 + HG], v3(sq_[:], HG), [sqr_], [K_("st8")])
                act(st8[:, gi * 4:gi * 4 + HG], st8[:, gi * 4:gi * 4 + HG], AF.Ln, [K_("st8")], [K_("st8")], bias=RMS_EPS, scale=1.0 / 128)
                act(st8[:, gi * 4:gi * 4 + HG], st8[:, gi * 4:gi * 4 + HG], AF.Exp, [K_("st8")], [K_("st8")], scale=-0.5)
                mixs = B_("mixs")
                tt("dve", mixs[:], v3(osb[:], HG), bc_last(st8[:, gi * 4:gi * 4 + HG], 128), ALU.mult, [osbr, K_("st8")], [K_("mixs")])
                tt("pool", mixs[:], mixs[:], v3(zg[:, c, h0 * 128:(h0 + HG) * 128], HG), ALU.mult, [K_("mixs"), ("zg", c, gi)], [K_("mixs")])
                ptt, ptr = get_pt()
                tr_group([(ptt[:, hh * 128:(hh + 1) * 128], mixs[:, hh, :], ident_b) for hh in range(HG)], [K_("mixs"), "cb"], [ptr])
                cp("act", mixT[:, h0:h0 + HG, cs], v3(ptt[:, 0:HG * 128], HG), [ptr], [("mixT", c, "g", gi)])

        for c in range(nch):
            streams = []
            for gi in range(NH_G // HG):
                pool_sel[0] = "G%d" % gi
                S.begin_stream()
                gdn_chain(gi, c)
                streams.append(S.end_stream())
            pool_sel[0] = "A"
            S.merge(streams[0], streams[1])

        S.mark('tile%d_C' % td_i[0])
        def rotary(psrc, psrcr, c, dst, dstr):
            xsb, xsbr = get_f()
            cp("act", xsb[:], psrc[:], [psrcr], [xsbr])
            xv = xsb[:].rearrange("p (h i two) -> p h i two", h=2, two=2)
            x0, x1 = xv[:, :, :, 0], xv[:, :, :, 1]
            dv = dst.rearrange("p (h i two) -> p h i two", h=2, two=2)
            cosb, sinb = bc_mid(cosT[:, c, :], 2), bc_mid(sinT[:, c, :], 2)
            t1, t1r = get_f()
            t2, t2r = get_f()
            a1, a2 = v3(t1[:, 0:256], 2), v3(t1[:, 256:512], 2)
            b1, b2 = v3(t2[:, 0:256], 2), v3(t2[:, 256:512], 2)
            tt("dve", a1, x0, cosb, ALU.mult, [xsbr, "cos"], [(t1r, 0)])
            tt("pool", a2, x1, sinb, ALU.mult, [xsbr, "sin"], [(t1r, 1)])
            tt("dve", dv[:, :, :, 0], a1, a2, ALU.subtract, [(t1r, 0), (t1r, 1)], [(dstr, 0)])
            tt("pool", b1, x1, cosb, ALU.mult, [xsbr, "cos"], [(t2r, 0)])
            tt("dve", b2, x0, sinb, ALU.mult, [xsbr, "sin"], [(t2r, 1)])
            tt("pool", dv[:, :, :, 1], b1, b2, ALU.add, [(t2r, 0), (t2r, 1)], [(dstr, 1)])
            return [(dstr, 0), (dstr, 1)]

        def ret_k(rp, c, slot, sres, rt, rtk):
            cs = slice(c * 128, (c + 1) * 128)
            pk, pkr = get_pa()
            mm_group([(pk[:, :], act16[:, kc, cs], slot[:, kc, :], kc == 0, kc == KC - 1) for kc in range(KC)],
                     [("hT", c, 0), ("hT", c, 1), sres], [pkr])
            rres = rotary(pk, pkr, c, rt[:], rtk)
            tt("dve", v3(kdr[:, c, :], 2), v3(rt[:], 2), bc_last(kdc[:, 2 * rp:2 * rp + 2], 256), ALU.mult,
               rres + ["cb"], [("kdr", c)])
            ptt, ptr = get_pt()
            tr_group([(ptt[:, i * 128:(i + 1) * 128], rt[:, i * 128:(i + 1) * 128], ident_b) for i in range(4)],
                     rres + ["cb"], [ptr])
            cp("act", krT[:, :, cs], v3(ptt[:, 0:512], 4), [ptr], [("krT", c)])

        def ret_v(rp, c, slot, sres):
            cs = slice(c * 128, (c + 1) * 128)
            pv, pvr = get_pa()
            mm_group([(pv[:, :], act16[:, kc, cs], slot[:, kc, :], kc == 0, kc == KC - 1) for kc in range(KC)],
                     [("hT", c, 0), ("hT", c, 1), sres], [pvr])
            cp("act", vr[:, c, :], pv[:], [pvr], [("vr", c)])

        def ret_q(rp, c, slot, sres, rt, rtk):
            cs = slice(c * 128, (c + 1) * 128)
            pq, pqr = get_pa()
            mm_group([(pq[:, :], act16[:, kc, cs], slot[:, kc, :], kc == 0, kc == KC - 1) for kc in range(KC)],
                     [("hT", c, 0), ("hT", c, 1), sres], [pqr])
            rres = rotary(pq, pqr, c, rt[:], rtk)
            ptt, ptr = get_pt()
            tr_group([(ptt[:, i * 128:(i + 1) * 128], rt[:, i * 128:(i + 1) * 128], ident_b) for i in range(4)],
                     rres + ["cb"], [ptr])
            cp("act", qrT[:, :, cs], v3(ptt[:, 0:512], 4), [ptr], [("qrT", c)])
            tt("dve", qdT[:, :, cs].rearrange("p (h two) i -> p h two i", h=2),
               ptt[:, 0:512].rearrange("p (h two i) -> p h two i", h=2, two=2),
               EBr[:, 2 * rp:2 * rp + 2, :].unsqueeze(2).to_broadcast([128, 2, 2, 128]), ALU.mult,
               [ptr, "cf"], [("qdT", c)])

        def ret_g(rp, c, slot, sres):
            cs = slice(c * 128, (c + 1) * 128)
            pg, pgr = get_pa()
            mm_group([(pg[:, :], act16[:, kc, cs], slot[:, kc, :], kc == 0, kc == KC - 1) for kc in range(KC)],
                     [("hT", c, 0), ("hT", c, 1), sres], [pgr])
            ft, ftr = get_f()
            act(ft[:], pg[:], AF.Silu, [pgr], [ftr])
            tt("pool", grw[:, c, :], ft[:], rgw[:, rp * 512:(rp + 1) * 512], ALU.mult, [ftr, "rgw"], [("grw", c)])

        for rp in range(2):
            pool_sel[0] = "H0"
            S.begin_stream()
            slot, sres = load_w(wi_v, OFF_KR + rp * 512, 512, 0, KC, wres["wi"])
            for c in range(nch):
                ret_k(rp, c, slot, sres, rot, "rot")
            if full:
                slot, sres = load_w(wi_v, OFF_VR + rp * 512, 512, 0, KC, wres["wi"])
                for c in range(nch):
                    ret_v(rp, c, slot, sres)
            sx = S.end_stream()
            pool_sel[0] = "H1"
            S.begin_stream()
            if full:
                slot, sres = load_w(wi_v, OFF_QR + rp * 512, 512, 0, KC, wres["wi"])
                for c in range(nch):
                    ret_q(rp, c, slot, sres, rot1, "rot1")
                slot, sres = load_w(wi_v, OFF_GR + rp * 512, 512, 0, KC, wres["wi"])
                for c in range(nch):
                    ret_g(rp, c, slot, sres)
            else:
                slot, sres = load_w(wi_v, OFF_VR + rp * 512, 512, 0, KC, wres["wi"])
                for c in range(nch):
                    ret_v(rp, c, slot, sres)
            sy = S.end_stream()
            pool_sel[0] = "A"
            S.merge(sx, sy)
            for c in range(nch):
                cs = slice(c * 128, (c + 1) * 128)
                SR = [("Srb", 2 * rp + hl) for hl in range(2)]
                if full:
                    pa_, par_ = get_pa()
                    items = []
                    for hl in range(2):
                        for half in range(2):
                            items.append((pa_[:, hl * 128:(hl + 1) * 128], krT[:, hl * 2 + half, cs], qrT[:, hl * 2 + half, cs],
                                          half == 0, half == 1))
                    mm_group(items, [("krT", c), ("qrT", c)], [par_])
                    tt("dve", attr[:], v3(pa_[:, 0:256], 2), DrT[:, 2 * rp:2 * rp + 2, :], ALU.mult, [par_, "cf"], ["attr"])
                    po, por = get_pa()
                    items = []
                    for hl in range(2):
                        h = 2 * rp + hl
                        oo = po[:, hl * 256:(hl + 1) * 256]
                        items.append((oo, qdT[:, hl * 2 + 0, cs], Srb[:, h, 0, :], True, False))
                        items.append((oo, qdT[:, hl * 2 + 1, cs], Srb[:, h, 1, :], False, False))
                        items.append((oo, attr[:, hl, :], vr[:, c, hl * 256:(hl + 1) * 256], False, True))
                    mm_group(items, [("qdT", c), "attr", ("vr", c), "Srb"] + SR, [por])
                    osb, osbr = get_f()
                    cp("act", osb[:], po[:], [por], [osbr])
                for hl in range(2):
                    h = 2 * rp + hl
                    pd, pdr = get_pa()
                    mm_group([(pd[:, half * 256:(half + 1) * 256], kdr[:, c, hl * 256 + half * 128:hl * 256 + (half + 1) * 128],
                               vr[:, c, hl * 256:(hl + 1) * 256], True, True) for half in range(2)],
                             [("kdr", c), ("vr", c)], [pdr])
                    stt("dve", Sr[:, h, :, :], Sr[:, h, :, :], float(gl_r[h]), v3(pd[:], 2), ALU.mult, ALU.add,
                        ["Sr", ("Sr", h), pdr], [("Sr", h)])
                    cp("act", Srb[:, h, :, :], Sr[:, h, :, :], ["Sr", ("Sr", h)], [("Srb", h)])
                if full:
                    sq_, sqr_ = get_f()
                    red("dve", st8[:, 8:10], v3(osb[:], 2), [osbr], ["st8b"])
                    tt("pool", sq_[:], osb[:], osb[:], ALU.mult, [osbr], [sqr_])
                    red("dve", st8[:, 10:12], v3(sq_[:], 2), [sqr_], ["st8c"])
                    ts("dve", st8[:, 8:10], st8[:, 8:10], 1.0 / 256, None, ALU.mult, None, ["st8b"], ["st8b"])
                    tt("dve", st8[:, 12:14], st8[:, 8:10], st8[:, 8:10], ALU.mult, ["st8b"], ["st8d"])
                    stt("dve", st8[:, 10:12], st8[:, 10:12], 1.0 / 256, st8[:, 12:14], ALU.mult, ALU.subtract,
                        ["st8c", "st8d"], ["st8c"])
                    act(st8[:, 10:12], st8[:, 10:12], AF.Ln, ["st8c"], ["st8c"], bias=GN_EPS, scale=1.0)
                    act(st8[:, 10:12], st8[:, 10:12], AF.Exp, ["st8c"], ["st8c"], scale=-0.5)
                    mixs = b512["mixr"]
                    mflat = mixs[:].rearrange("p a b -> p (a b)")
                    for hl in range(2):
                        ts("dve", mflat[:, hl * 256:(hl + 1) * 256], osb[:, hl * 256:(hl + 1) * 256], st8[:, 8 + hl:9 + hl],
                           st8[:, 10 + hl:11 + hl], ALU.subtract, ALU.mult, [osbr, "st8b", "st8c", "mixr"], [("mixr", hl)])
                    tt("pool", mflat, mflat, grw[:, c, :], ALU.mult,
                       ["mixr", ("mixr", 0), ("mixr", 1), ("grw", c)], ["mixr"])
                    ptt, ptr = get_pt()
                    tr_group([(ptt[:, i * 128:(i + 1) * 128], mixs[:, i, :], ident_b) for i in range(4)], ["mixr", "cb"], [ptr])
                    cp("act", mixT[:, 8 + rp * 4:8 + rp * 4 + 4, cs], v3(ptt[:, 0:512], 4), [ptr], [("mixT", c, "r", rp)])

        if full:
            MX = [("mixT", c, "g", gi) for c in range(nch) for gi in range(NH_G // HG)] + \
                 [("mixT", c, "r", rp) for c in range(nch) for rp in range(2)]
            S.mark('tile%d_D' % td_i[0])
            for nb in range(4):
                slot, sres = load_w(wo_v, nb * 512, 512, 0, KC, wres["wo"])
                for c in range(nch):
                    cs = slice(c * 128, (c + 1) * 128)
                    pw_, pwr_ = get_pa()
                    mm_group([(pw_[:, :], mixT[:, kc, cs], slot[:, kc, :], kc == 0, kc == KC - 1) for kc in range(KC)],
                             MX + [sres], [pwr_])
                    tt("dve", xt[:, c, nb * 512:(nb + 1) * 512], xt[:, c, nb * 512:(nb + 1) * 512], pw_[:], ALU.add,
                       [("xt", c), pwr_], [("xt", c)])
            rms_to_act16(nch, fnw, "fnw")
            HT2 = hT_res(nch)
            S.mark('tile%d_E' % td_i[0])
            for part in range(NPART):
                g0 = part * GPP
                done = 0
                while done < GPP:
                    ng = min(4, GPP - done)
                    sg_, sgr_ = load_w(wg_v, (g0 + done) * 128, ng * 128, 0, KC, wres["wg"])
                    su_, sur_ = load_w(wg_v, DFF + (g0 + done) * 128, ng * 128, 0, KC, wres["wg"])
                    fts = []
                    for gi_ in range(ng):
                        pg_, pgr_ = get_pa()
                        mm_group([(pg_[:, 0:T], sg_[:, kc, gi_ * 128:(gi_ + 1) * 128], act16[:, kc, 0:T], kc == 0, kc == KC - 1)
                                  for kc in range(KC)], HT2 + [sgr_], [pgr_])
                        ft, ftr = get_f()
                        act(ft[:, 0:T], pg_[:, 0:T], AF.Silu, [pgr_], [ftr])
                        fts.append((ft, ftr))
                    for gi_ in range(ng):
                        ft, ftr = fts[gi_]
                        pu_, pur_ = get_pa()
                        mm_group([(pu_[:, 0:T], su_[:, kc, gi_ * 128:(gi_ + 1) * 128], act16[:, kc, 0:T], kc == 0, kc == KC - 1)
                                  for kc in range(KC)], HT2 + [sur_], [pur_])
                        tt("dve", actT[:, done + gi_, 0:T], ft[:, 0:T], pu_[:, 0:T], ALU.mult, [ftr, pur_], [("actT", done + gi_)])
                    done += ng
                AT_ = [("actT", i) for i in range(GPP)]
                for nb in range(4):
                    sd_, sdr_ = load_w(wd_v, nb * 512, 512, g0, GPP, wres["wd"])
                    for c in range(nch):
                        cs = slice(c * 128, (c + 1) * 128)
                        pd_, pdr_ = get_pa()
                        mm_group([(pd_[:, :], actT[:, i, cs], sd_[:, i, :], i == 0, i == GPP - 1) for i in range(GPP)],
                                 AT_ + [sdr_], [pdr_])
                        tt("dve", xt[:, c, nb * 512:(nb + 1) * 512], xt[:, c, nb * 512:(nb + 1) * 512], pd_[:], ALU.add,
                           [("xt", c), pdr_], [("xt", c)])
            S.mark('tile%d_F' % td_i[0])
            MXALL = [("mixT", c_, "g", gi_) for c_ in range(NCH) for gi_ in range(NH_G // HG)] + \
                    [("mixT", c_, "r", rp_) for c_ in range(NCH) for rp_ in range(2)]
            finw_v = mixT[:].rearrange("p a b -> p (a b)").bitcast(F32)
            dma("sp", finw_v, finw_d[:], (), MXALL + ["finw"], "finw")
            memset("dve", ss4[:], 0.0, ["ss4"])
            for c in range(nch):
                xr = ("xt", c)
                for q in range(4):
                    act(junk[:], xt[:, c, q * 512:(q + 1) * 512], AF.Square, [xr, "ss4"], ["junk", ("ss4", c, q)],
                        accum_out=ss4[:, c * 4 + q:c * 4 + q + 1])
                red("dve", ssum[:, c:c + 1], ss4[:, c * 4:c * 4 + 4], [("ss4", c, q) for q in range(4)], [("ssum", c)])
                act(rstd[:, c:c + 1], ssum[:, c:c + 1], AF.Ln, [("ssum", c)], [("rstd", c)], bias=RMS_EPS, scale=1.0 / D)
                act(rstd[:, c:c + 1], rstd[:, c:c + 1], AF.Exp, [("rstd", c)], [("rstd", c)], scale=-0.5)
                stt("dve", xt[:, c, :], xt[:, c, :], rstd[:, c:c + 1], finw_v, ALU.mult, ALU.mult, [xr, ("rstd", c), "finw"] + MXALL, [xr])
                if samp:
                    dma("pool", td["y"][0:DEC_SEQ, :], xt[0:DEC_SEQ, c, :], [xr], [("yout", c)], ("yst", c))
                else:
                    dma("pool", td["y"][c * 128:(c + 1) * 128, :], xt[:, c, :], [xr], [("yout", c)], ("yst", c))

    def store_states(oc, og, orr, tag):
        S.mark('store_' + tag)
        allSg = ["Sg"] + [("Sg", h) for h in range(NH_G)]
        allSr = ["Sr"] + [("Sr", h) for h in range(NH_R)]
        allH = ["hist"] + [("hist", g) for g in range(24)]
        dma("pool", og.rearrange("h k v -> k h v"), Sg[:], allSg, [("og", tag)], ("og", tag))
        dma("pool", orr.rearrange("h (two p) v -> p h two v", p=128), Sr[:], allSr, [("or", tag)], ("or", tag))
        ph, phr = get_pa()
        tr_group([(ph[0:72, 0:128], hist[:].rearrange("p j g -> p (j g)"), ident_f)], allH + ["cf"], [phr])
        cp("act", h72[:], ph[0:72, 0:128], [phr], ["h72"])
        for j in range(3):
            dma("pool", oc[j, :].rearrange("(g p) -> g p", p=128), h72[j * 24:(j + 1) * 24, :], ["h72"],
                [("oc", tag, j)], ("oc", tag, j))

    def reset_states():
        allSg = ["Sg"] + [("Sg", h) for h in range(NH_G)]
        allSr = ["Sr"] + [("Sr", h) for h in range(NH_R)]
        allH = ["hist"] + [("hist", g) for g in range(24)]
        memset("pool", Sg[:], 0.0, allSg)
        memset("pool", Sr[:], 0.0, allSr)
        memset("pool", hist[:], 0.0, allH)
        memset("dve", Sgb[:], 0.0, ["Sgb"] + [("Sgb", h) for h in range(NH_G)])
        memset("dve", Srb[:], 0.0, ["Srb"] + [("Srb", h) for h in range(NH_R)])

    lg = [math.log(1.0 - 2.0 ** (-5.0 - h)) for h in range(NH_R)]
    gl_full = [math.exp(l * C) for l in lg]
    gl_samp = [math.exp(l * DEC_SEQ) for l in lg]

    emit_tile(dict(nch=1, full=True, L=DEC_SEQ, samp=True, x=xs_d, tok0=0, qhist=False, gl_r=gl_samp, y=ys_d))
    store_states(ocs_d, ogs_d, ors_d, "s")
    reset_states()
    for t in range(n_pre):
        emit_tile(dict(nch=NCH, full=False, L=TT, samp=False, x=xp_d[t * TT:(t + 1) * TT, :], tok0=128 + t * TT,
                       qhist=(t == n_pre - 1), gl_r=gl_full, y=None))
    for t in range(n_own):
        emit_tile(dict(nch=NCH, full=True, L=TT, samp=False, x=xo_d[t * TT:(t + 1) * TT, :], tok0=128 + (n_pre + t) * TT,
                       qhist=False, gl_r=gl_full, y=yo_d[t * TT:(t + 1) * TT, :]))
    store_states(ocp_d, ogp_d, orp_d, "p")
    outs = [("og", "s"), ("or", "s"), ("og", "p"), ("or", "p")] + [("oc", tg, j) for tg in "sp" for j in range(3)] + \
           [("yout", c) for c in range(NCH)]
    S.add("pool", None, reads=outs, writes=())

    if trunc is not None:
        S.truncate(trunc)
    build_program.marks = S.marks
    build_program.sb_bytes = sb_bytes[0]
    build_program.lines = [(o.id, o.eng, getattr(o, 'line', -1)) for o in S.all]
    semkeys = S.finalize()
    build_program.nsem = len(semkeys)
    sems = {k: es.enter_context(nc.semaphore("s%d" % i)) for i, k in enumerate(semkeys)}
    with nc.Block() as block:
        def run(eng_name, e):
            for op in S.ops[eng_name]:
                for k, v in op.waits:
                    e.wait_ge(sems[k], v)
                if op.fn is None:
                    continue
                inst = op.fn(e)
                if op.signal:
                    k, _ = op.sigval
                    inst.then_inc(sems[k], 16 if op.dma is not None else 1)

        @block.tensor
        def _(e):
            run("pe", e)

        @block.scalar
        def _(e):
            run("act", e)

        @block.vector
        def _(e):
            run("dve", e)

        @block.gpsimd
        def _(e):
            run("pool", e)

        @block.sync
        def _(e):
            run("sp", e)
    es.close()
    return nc


CF_COLS = 128 * 4 + 512 + 512 + 4 + 4 + 1 + 16 + 128
CB_COLS = 256 + 5 * 128 + 8


def _consts():
    cf = np.zeros((128, CF_COLS), np.float32)
    j = np.arange(128)[:, None]
    i = np.arange(128)[None, :]
    o = 0
    cf[:, o:o + 128] = (i >= j); o += 128
    cf[:, o:o + 128] = (i > j); o += 128
    cf[:, o:o + 128] = np.eye(128); o += 128
    cf[:, o:o + 128] = 1.0; o += 128
    lg = np.log(1.0 - 2.0 ** (-5.0 - np.arange(NH_R, dtype=np.float64)))
    for h in range(NH_R):
        cf[:, o + h * 128:o + (h + 1) * 128] = np.where(i >= j, np.exp(lg[h] * (i - j)) * (256.0 ** -0.5), 0.0)
    o += 512
    for h in range(NH_R):
        cf[:, o + h * 128:o + (h + 1) * 128] = np.exp(lg[h] * (i + 1.0))
    o += 512
    jj = np.arange(128)
    for h in range(NH_R):
        cf[:, o + h] = np.exp(lg[h] * (C - 1 - jj)) * (256.0 ** -0.5)
    o += 4
    for h in range(NH_R):
        cf[:, o + h] = np.where(jj < DEC_SEQ, np.exp(lg[h] * (DEC_SEQ - 1 - np.minimum(jj, DEC_SEQ - 1))) * (256.0 ** -0.5), 0.0)
    o += 4
    cf[:, o] = (jj < DEC_SEQ); o += 1
    cf[0:16, o:o + 16] = np.eye(16); o += 16
    cf[0:16, o:o + 128] = 1.0; o += 128
    assert o == CF_COLS
    cb = np.zeros((128, CB_COLS), np.float32)
    cb[:, 0:128] = np.eye(128)
    cb[:, 128:256] = 1.0
    ii = np.arange(128)[:, None]
    jj2 = np.arange(128)[None, :]
    bd = lambda b: ((ii // b) == (jj2 // b)).astype(np.float32)
    cb[:, 256:384] = bd(8)
    for li, b in enumerate((8, 16, 32, 64)):
        cb[:, 384 + li * 128:512 + li * 128] = bd(2 * b) - bd(b)
    cb[:, 896:904] = cf[:, 128 * 4 + 1024:128 * 4 + 1032]
    return cf, cb.astype(ml_dtypes.bfloat16)


def _rope_tables(pos):
    d2 = 128
    inv = (1.0 / (10000.0 ** np.linspace(0.0, 1.0, d2, dtype=np.float32))).astype(np.float32)
    ang = pos.astype(np.float32)[:, None] * inv[None, :]
    return np.cos(ang).astype(np.float32), np.sin(ang).astype(np.float32)


_PROG_CACHE = {}


def run_cores(inputs, n_pre, n_own, seq_half):
    f32 = np.float32
    x_prompt = np.asarray(inputs["x_prompt"], f32)
    x_sample = np.asarray(inputs["x_sample"], f32)
    cf, cb = _consts()
    key = (n_pre, n_own)
    if key not in _PROG_CACHE:
        _PROG_CACHE[key] = build_program(n_pre, n_own)
    nc = _PROG_CACHE[key]
    w_in = np.ascontiguousarray(inputs["w_in"][0], f32)
    w_out = np.ascontiguousarray(inputs["w_out"][0], f32)
    w_gu = np.ascontiguousarray(inputs["w_gate_up"][0], f32)
    w_dn = np.ascontiguousarray(inputs["w_down"][0], f32)
    col = lambda v: np.ascontiguousarray(np.asarray(v, f32).reshape(KC, 128).T)
    bcast = lambda v: np.ascontiguousarray(np.broadcast_to(np.asarray(v, f32)[None, :], (128, v.shape[-1])))
    common = dict(
        w_in=w_in, w_out=w_out, w_gu=w_gu, w_dn=w_dn,
        anw=col(inputs["attn_norm_w"][0]), fnw=col(inputs["ffn_norm_w"][0]), finw=bcast(np.asarray(inputs["final_norm_w"])),
        convw=np.ascontiguousarray(np.asarray(inputs["conv_w"][0], f32).reshape(4, 24, 128).transpose(2, 1, 0)),
        alog=bcast(np.asarray(inputs["a_log"][0])), dtb=bcast(np.asarray(inputs["dt_bias"][0])),
        gnw=bcast(np.asarray(inputs["gdn_norm_w"][0])), rgw=bcast(np.asarray(inputs["ret_gn_w"][0])),
        cf32=cf, cbf16=cb,
    )
    in_maps = []
    for c in range(8):
        b, half = c // 2, c % 2
        xs = np.zeros((128, D), f32)
        xs[:DEC_SEQ] = x_sample[c]
        xo = np.ascontiguousarray(x_prompt[b, half * seq_half:(half + 1) * seq_half])
        if half == 1:
            xp = np.ascontiguousarray(x_prompt[b, 0:seq_half])
        else:
            xp = np.zeros((seq_half, D), f32)
        pos = np.concatenate([PAST_LEN + np.arange(128), np.arange(seq_half), half * seq_half + np.arange(seq_half)])
        cosd, sind = _rope_tables(pos)
        sg0 = np.ascontiguousarray(np.asarray(inputs["state_gdn"][0, c], f32).transpose(1, 0, 2))
        sr0 = np.ascontiguousarray(np.asarray(inputs["state_ret"][0, c], f32).reshape(NH_R, 2, 128, 256).transpose(2, 0, 1, 3))
        hist0 = np.ascontiguousarray(np.asarray(inputs["state_gdn_conv"][0, c], f32).reshape(3, 24, 128).transpose(2, 0, 1))
        m = dict(common)
        m.update(xs=xs, xp=xp, xo=xo, cosd=cosd, sind=sind, sg0=sg0, sr0=sr0, hist0=hist0)
        in_maps.append(m)
    res = run_bass_kernel_spmd(nc, in_maps, core_ids=list(range(8)))
    return res.results


def kernel(x_prompt, x_sample, state_gdn_conv, state_gdn, state_ret, attn_norm_w, w_in, conv_w, a_log, dt_bias,
           gdn_norm_w, ret_gn_w, w_out, ffn_norm_w, w_gate_up, w_down, final_norm_w):
    inputs = dict(x_prompt=x_prompt, x_sample=x_sample, state_gdn_conv=state_gdn_conv, state_gdn=state_gdn,
                  state_ret=state_ret, attn_norm_w=attn_norm_w, w_in=w_in, conv_w=conv_w, a_log=a_log, dt_bias=dt_bias,
                  gdn_norm_w=gdn_norm_w, ret_gn_w=ret_gn_w, w_out=w_out, ffn_norm_w=ffn_norm_w, w_gate_up=w_gate_up,
                  w_down=w_down, final_norm_w=final_norm_w)
    inputs = {k: np.asarray(v) for k, v in inputs.items()}
    B, SEQ = inputs["x_prompt"].shape[0], inputs["x_prompt"].shape[1]
    seq_half = SEQ // 2
    n_t = seq_half // TT
    r = run_cores(inputs, n_t, n_t, seq_half)
    f32 = np.float32
    y_prompt = np.zeros((B, SEQ, D), f32)
    for c in range(8):
        b, half = c // 2, c % 2
        y_prompt[b, half * seq_half:(half + 1) * seq_half] = r[c]["y_o"]
    y_sample = np.stack([r[c]["y_s"] for c in range(8)]).astype(f32)
    pc = np.stack([r[2 * b + 1]["oc_p"] for b in range(B)])[None].astype(f32)
    pg = np.stack([r[2 * b + 1]["og_p"] for b in range(B)])[None].astype(f32)
    pr = np.stack([r[2 * b + 1]["or_p"] for b in range(B)])[None].astype(f32)
    sc = np.stack([r[c]["oc_s"] for c in range(8)])[None].astype(f32)
    sg = np.stack([r[c]["og_s"] for c in range(8)])[None].astype(f32)
    sr = np.stack([r[c]["or_s"] for c in range(8)])[None].astype(f32)
    return (y_prompt, y_sample, pc, pg, pr, sc, sg, sr)
```

```python
import math
from contextlib import ExitStack

import numpy as np
import ml_dtypes

import concourse.bass as bass
import concourse.mybir as mybir
from concourse.bass_utils import run_bass_kernel_spmd

F32 = mybir.dt.float32
BF16 = mybir.dt.bfloat16
AF = mybir.ActivationFunctionType
ALU = mybir.AluOpType
AX = mybir.AxisListType

D = 2048
KC = 16
PW = 8208
DFF = 5632
NH_G = 8
NH_R = 4
C = 128
TT = 256
NCH = TT // 128
PAST_LEN = 4096
DEC_SEQ = 16
RMS_EPS = 1e-6
GN_EPS = 1e-5
L2_EPS = 1e-6
HG = 4
NPART = 4
GPP = 11

OFF_Q, OFF_K, OFF_V, OFF_Z, OFF_BA, OFF_QR, OFF_KR, OFF_VR, OFF_GR = (
    0, 1024, 2048, 3072, 4096, 4112, 5136, 6160, 7184)


class _Op:
    __slots__ = ("eng", "fn", "dma", "id", "deps", "epoch", "sigval", "signal", "waits", "line")


def _caller_line():
    import sys
    f = sys._getframe(2)
    while f is not None and f.f_code.co_name in ('add', 'dma', 'mm_group', 'tr_group', 'act', 'tt', 'ts', 'stt', 'cp', 'red', 'memset'):
        f = f.f_back
    return f.f_lineno if f is not None else -1


class Sched:
    ENGS = ("pe", "act", "dve", "pool", "sp")

    def __init__(self):
        self.ops = {e: [] for e in self.ENGS}
        self.all = []
        self.lastw = {}
        self.readers = {}
        self.epoch = 0
        self.marks = []
        self.stream = None

    def add(self, eng, fn, reads=(), writes=(), dma=None):
        if self.stream is not None:
            self.stream.append((eng, fn, list(reads), list(writes), dma, _caller_line()))
            return None
        return self._add(eng, fn, reads, writes, dma, _caller_line())

    def begin_stream(self):
        self.stream = []

    def end_stream(self):
        st, self.stream = self.stream, None
        return st

    def merge(self, a, b, emit=True):
        na, nb = len(a), len(b)
        ia = ib = 0
        out = []
        while ia < na or ib < nb:
            if ib >= nb or (ia < na and ia * nb <= ib * na):
                out.append(a[ia]); ia += 1
            else:
                out.append(b[ib]); ib += 1
        if emit:
            for r in out:
                self._add(*r)
        return out

    def _add(self, eng, fn, reads, writes, dma, line):
        op = _Op()
        op.eng, op.fn, op.dma = eng, fn, dma
        op.id = len(self.all)
        op.epoch = self.epoch
        op.signal = False
        op.line = line
        deps = set()
        raw = set()
        ps_r = [r for r in reads if isinstance(r, tuple) and r[0] in ("pa", "pt")]
        if ps_r:
            reads = [r for r in reads if r not in ps_r]
            writes = list(writes) + ps_r
            for r in ps_r:
                w = self.lastw.get(r)
                if w is not None:
                    raw.add(w)
        for r in reads:
            w = self.lastw.get(r)
            if w is not None:
                deps.add(w)
                raw.add(w)
        for w_ in writes:
            w = self.lastw.get(w_)
            if w is not None:
                deps.add(w)
            for rid in self.readers.get(w_, ()):
                deps.add(rid)
        keep = []
        for d in deps:
            if d == op.id:
                continue
            dop = self.all[d]
            if dop.dma is None and dop.eng == eng:
                if eng == "pe":
                    continue
                if d not in raw:
                    continue
            keep.append(d)
        op.deps = keep
        for r in reads:
            lst = self.readers.setdefault(r, [])
            if dma is None:
                lst[:] = [x for x in lst if not (self.all[x].dma is None and self.all[x].eng == eng)]
            lst.append(op.id)
        for w_ in writes:
            self.lastw[w_] = op.id
            self.readers[w_] = []
        self.all.append(op)
        self.ops[eng].append(op)
        return op

    def mark(self, name):
        self.marks.append((name, len(self.all)))

    def truncate(self, n):
        self.all = self.all[:n]
        for e in self.ENGS:
            self.ops[e] = [o for o in self.ops[e] if o.id < n]
        op = _Op()
        op.eng, op.fn, op.dma = "sp", None, None
        op.id = len(self.all)
        op.epoch = self.epoch
        op.signal = False
        op.deps = [o.id for o in self.all if o.dma is not None]
        self.all.append(op)
        self.ops["sp"].append(op)

    def finalize(self):
        for op in self.all:
            for d in op.deps:
                self.all[d].signal = True
        cnt = {}
        dcnt = {}
        semkeys = []
        for op in self.all:
            if op.dma is not None:
                k = ("dma", op.dma)
                dcnt[k] = dcnt.get(k, 0) + 16
                op.sigval = (k, dcnt[k])
                op.signal = True
                if k not in semkeys:
                    semkeys.append(k)
            elif op.signal:
                k = ("eng", op.eng, op.epoch)
                cnt[k] = cnt.get(k, 0) + 1
                op.sigval = (k, cnt[k])
                if k not in semkeys:
                    semkeys.append(k)
        for e in self.ENGS:
            seen = {}
            for op in self.ops[e]:
                waits = []
                for d in op.deps:
                    k, v = self.all[d].sigval
                    if seen.get(k, 0) >= v:
                        continue
                    seen[k] = v
                    waits.append((k, v))
                best = {}
                for k, v in waits:
                    best[k] = max(best.get(k, 0), v)
                op.waits = list(best.items())
        return semkeys


def build_program(n_pre, n_own, debug=None, trunc=None):
    nc = bass.Bass("TRN2", target_bir_lowering=False)
    es = ExitStack()
    S = Sched()

    def dram(name, shape, dt, kind):
        return nc.dram_tensor(name, list(shape), dt, kind=kind).ap()

    NTOK = 128 + (n_pre + n_own) * TT
    xs_d = dram("xs", [128, D], F32, "ExternalInput")
    xp_d = dram("xp", [max(n_pre, 1) * TT, D], F32, "ExternalInput")
    xo_d = dram("xo", [n_own * TT, D], F32, "ExternalInput")
    cos_d = dram("cosd", [NTOK, 128], F32, "ExternalInput")
    sin_d = dram("sind", [NTOK, 128], F32, "ExternalInput")
    sg0_d = dram("sg0", [128, NH_G, 128], F32, "ExternalInput")
    sr0_d = dram("sr0", [128, NH_R, 2, 256], F32, "ExternalInput")
    hist0_d = dram("hist0", [128, 3, 24], F32, "ExternalInput")
    w_in_d = dram("w_in", [D, PW], F32, "ExternalInput")
    w_out_d = dram("w_out", [D, D], F32, "ExternalInput")
    w_gu_d = dram("w_gu", [D, 2 * DFF], F32, "ExternalInput")
    w_dn_d = dram("w_dn", [DFF, D], F32, "ExternalInput")
    anw_d = dram("anw", [128, KC], F32, "ExternalInput")
    fnw_d = dram("fnw", [128, KC], F32, "ExternalInput")
    finw_d = dram("finw", [128, D], F32, "ExternalInput")
    convw_d = dram("convw", [128, 24, 4], F32, "ExternalInput")
    alog_d = dram("alog", [128, NH_G], F32, "ExternalInput")
    dtb_d = dram("dtb", [128, NH_G], F32, "ExternalInput")
    gnw_d = dram("gnw", [128, 128], F32, "ExternalInput")
    rgw_d = dram("rgw", [128, 1024], F32, "ExternalInput")
    cf_d = dram("cf32", [128, CF_COLS], F32, "ExternalInput")
    cb_d = dram("cbf16", [128, CB_COLS], BF16, "ExternalInput")
    ys_d = dram("y_s", [DEC_SEQ, D], F32, "ExternalOutput")
    yo_d = dram("y_o", [n_own * TT, D], F32, "ExternalOutput")
    ocs_d = dram("oc_s", [3, 3072], F32, "ExternalOutput")
    ogs_d = dram("og_s", [NH_G, 128, 128], F32, "ExternalOutput")
    ors_d = dram("or_s", [NH_R, 256, 256], F32, "ExternalOutput")
    ocp_d = dram("oc_p", [3, 3072], F32, "ExternalOutput")
    ogp_d = dram("og_p", [NH_G, 128, 128], F32, "ExternalOutput")
    orp_d = dram("or_p", [NH_R, 256, 256], F32, "ExternalOutput")
    wi_b = dram("wi_b", [D, PW], BF16, "Internal")
    wo_b = dram("wo_b", [D, D], BF16, "Internal")
    wg_b = dram("wg_b", [D, 2 * DFF], BF16, "Internal")
    wd_b = dram("wd_b", [DFF, D], BF16, "Internal")
    wi_v = wi_b.rearrange("(kc p) n -> p kc n", p=128)
    wo_v = wo_b.rearrange("(kc p) n -> p kc n", p=128)
    wg_v = wg_b.rearrange("(kc p) n -> p kc n", p=128)
    wd_v = wd_b.rearrange("(kc p) n -> p kc n", p=128)

    sb_bytes = [0]

    def sb(name, shape, dt):
        n = 1
        for x in shape[1:]:
            n *= x
        sb_bytes[0] += n * (4 if dt == F32 else 2)
        return es.enter_context(nc.sbuf_tensor(name, list(shape), dt))

    def ps(name, shape, dt):
        return es.enter_context(nc.psum_tensor(name, list(shape), dt))

    xt = sb("xt", [128, NCH, D], F32)
    act16 = sb("act16", [128, KC, TT], BF16)
    mixT = sb("mixT", [128, KC, TT], BF16)
    NSLOT = 3
    ring = [sb(f"ring{i}", [128, KC, 512], BF16) for i in range(NSLOT)]
    wba = sb("wba", [128, KC, 16], BF16)
    Sg = sb("Sg", [128, NH_G, 128], F32)
    Sgb = sb("Sgb", [128, NH_G, 128], BF16)
    Sr = sb("Sr", [128, NH_R, 2, 256], F32)
    Srb = sb("Srb", [128, NH_R, 2, 256], BF16)
    hist = sb("hist", [128, 3, 24], F32)
    cf = sb("cf", [128, CF_COLS], F32)
    cb = sb("cb", [128, CB_COLS], BF16)
    anw = sb("anw_s", [128, KC], F32)
    fnw = sb("fnw_s", [128, KC], F32)
    convw = sb("convw_s", [128, 24, 4], F32)
    alog = sb("alog_s", [128, NH_G], F32)
    dtb = sb("dtb_s", [128, NH_G], F32)
    nega = sb("nega", [128, NH_G], F32)
    gnw = sb("gnw_s", [128, 128], F32)
    rgw = sb("rgw_s", [128, 1024], F32)
    cosT = sb("cosT", [128, NCH, 128], F32)
    sinT = sb("sinT", [128, NCH, 128], F32)
    junk = sb("junk", [128, 512], BF16)
    ss4 = sb("ss4", [128, 16], F32)
    ssum = sb("ssum", [128, 4], F32)
    rstd = sb("rstd", [128, 4], F32)
    hb = [sb(f"hb{i}", [128, D], BF16) for i in range(1)]
    ba = sb("ba", [128, NCH, 16], F32)
    rowsrc = sb("rowsrc", [128, NCH, 16], F32)
    tm8 = [sb(f"tm8_{i}", [128, NCH, 8], F32) for i in range(3)]
    gtm = sb("gtm", [128, NCH, 8], F32)
    gtot = sb("gtot", [128, NCH, 8], F32)
    bge = sb("bge", [128, NCH, 8], F32)
    dks = sb("dks", [128, NCH, 8], F32)
    glb = sb("glb", [128, NCH, 8], F32)
    pre = [sb(f"pre{i}", [128, TT + 3], F32) for i in range(2)]
    cacc = [sb(f"cacc{i}", [128, TT], F32) for i in range(2)]
    qs = [sb(f"qs{i}", [128, TT], F32) for i in range(2)]
    sqb = [sb(f"sqb{i}", [128, TT], BF16) for i in range(2)]
    rsb = [sb(f"rsb{i}", [128, TT], F32) for i in range(2)]
    kTn = sb("kTn", [128, NH_G, TT], BF16)
    qTn = sb("qTn", [128, NH_G, TT], BF16)
    vT = sb("vT", [128, NH_G, TT], BF16)
    zg = sb("zg", [128, NCH, 1024], BF16)
    f512 = [sb(f"f512_{i}", [128, 512], F32) for i in range(6)]
    Dtig = [sb(f"Dti{i}", [128, HG, 128], F32) for i in range(2)]
    Dtsg = [sb(f"Dts{i}", [128, HG, 128], F32) for i in range(2)]
    PHYS = {"AT": "P1", "kb": "P0", "vb": "PT1", "kbg": "IpP", "kdec": "P0", "kcdT": "P1", "u": "PT0", "mixs": "A"}
    b512g = [{n: sb("b%d_%s" % (i, n), [128, HG, 128], BF16) for n in
              ("A", "P0", "P1", "PT0", "PT1", "IpP", "RT0", "RT1", "qd", "attnT")} for i in range(2)]
    b512 = {"mixr": sb("b_mixr", [128, HG, 128], BF16)}
    krT = sb("krT", [128, 4, TT], BF16)
    qrT = sb("qrT", [128, 4, TT], BF16)
    qdT = sb("qdT", [128, 4, TT], BF16)
    kdr = sb("kdr", [128, NCH, 512], BF16)
    vr = sb("vr", [128, NCH, 512], BF16)
    grw = sb("grw", [128, NCH, 512], BF16)
    rot = sb("rot", [128, 512], BF16)
    rot1 = sb("rot1", [128, 512], BF16)
    attr = sb("attr", [128, 2, 128], BF16)
    st8 = sb("st8", [128, 16], F32)
    actT = sb("actT", [128, GPP, TT], BF16)
    h72 = sb("h72", [72, 128], F32)
    pa = [ps(f"pa{i}", [128, 512], F32) for i in range(6)]
    pt = [ps(f"pt{i}", [128, 1024], BF16) for i in range(2)]
    rr = {}
    pool_sel = ["A"]
    POOLS = {
        "pa": {"A": list(range(6)), "G0": [0, 1, 2], "G1": [3, 4, 5], "H0": [0, 1], "H1": [2, 3], "Z": [4, 5]},
        "pt": {"A": [0, 1], "G0": [0], "G1": [1], "H0": [0], "H1": [1], "Z": [0]},
        "f512": {"A": list(range(6)), "G0": [0, 1, 2], "G1": [3, 4, 5], "H0": [0, 1, 2], "H1": [3, 4, 5], "Z": [0]},
        "ring": {"A": [0, 1, 2], "H0": [0, 1], "H1": [2]},
    }

    def _rr(kind):
        sel = pool_sel[0] if pool_sel[0] in POOLS[kind] else "A"
        lst = POOLS[kind][sel]
        k = (kind, sel)
        i = rr.get(k, 0)
        rr[k] = (i + 1) % len(lst)
        return lst[i]

    def get_pa():
        i = _rr("pa")
        return pa[i], ("pa", i)

    def get_pt():
        i = _rr("pt")
        return pt[i], ("pt", i)

    def get_f():
        i = _rr("f512")
        return f512[i], ("f512", i)

    def get_ring():
        i = _rr("ring")
        return ring[i], ("ring", i)

    o = 0
    def cfv(n):
        nonlocal o
        v = cf[:, o:o + n]
        o += n
        return v
    mask_incl = cfv(128)
    mask_strict = cfv(128)
    ident_f = cfv(128)
    ones_f = cfv(128)
    DrT = cfv(512).rearrange("p (h i) -> p h i", h=4)
    EBr = cfv(512).rearrange("p (h i) -> p h i", h=4)
    _kf = cfv(4)
    _ks = cfv(4)
    vm_samp = cfv(1)
    i16 = cf[0:16, o:o + 16]; o += 16
    ones16 = cf[0:16, o:o + 128]; o += 128
    assert o == CF_COLS
    ident_b = cb[:, 0:128]
    ones_b = cb[:, 128:256]
    bd8 = cb[:, 256:384]
    kdc_full = cb[:, 896:900]
    kdc_samp = cb[:, 900:904]
    mX = [cb[:, 384 + i * 128:512 + i * 128] for i in range(4)]

    def bc_mid(ap2, n):
        return ap2.unsqueeze(1).to_broadcast([ap2.shape[0], n, ap2.shape[1]])

    def bc_last(ap2, n):
        return ap2.unsqueeze(2).to_broadcast([ap2.shape[0], ap2.shape[1], n])

    def v3(ap2, a):
        return ap2.rearrange("p (a b) -> p a b", a=a)

    def dma(q, out, in_, reads, writes, key):
        S.add(q, lambda e: e.dma_start(out=out, in_=in_), reads=reads, writes=writes, dma=key)

    def mm_group(items, reads, writes):
        def fn(e):
            last = None
            for (o_, l_, r_, st, sp) in items:
                last = e.matmul(o_, lhsT=l_, rhs=r_, start=st, stop=sp)
            return last
        S.add("pe", fn, reads=reads, writes=writes)

    def tr_group(items, reads, writes):
        def fn(e):
            last = None
            for (o_, i_, id_) in items:
                last = e.transpose(out=o_, in_=i_, identity=id_)
            return last
        S.add("pe", fn, reads=reads, writes=writes)

    def act(out, in_, func, reads, writes, bias=None, scale=None, accum_out=None):
        kw = {}
        if bias is not None:
            kw["bias"] = bias
        if scale is not None:
            kw["scale"] = scale
        if accum_out is not None:
            kw["accum_out"] = accum_out
        S.add("act", lambda e: e.activation(out=out, in_=in_, func=func, **kw), reads=reads, writes=writes)

    def tt(eng, out, in0, in1, op, reads, writes):
        S.add(eng, lambda e: e.tensor_tensor(out=out, in0=in0, in1=in1, op=op), reads=reads, writes=writes)

    def ts(eng, out, in0, s1, s2, op0, op1, reads, writes):
        if s2 is None:
            S.add(eng, lambda e: e.tensor_scalar(out=out, in0=in0, scalar1=s1, scalar2=None, op0=op0),
                  reads=reads, writes=writes)
        else:
            S.add(eng, lambda e: e.tensor_scalar(out=out, in0=in0, scalar1=s1, scalar2=s2, op0=op0, op1=op1),
                  reads=reads, writes=writes)

    def stt(eng, out, in0, scalar, in1, op0, op1, reads, writes):
        S.add(eng, lambda e: e.scalar_tensor_tensor(out=out, in0=in0, scalar=scalar, in1=in1, op0=op0, op1=op1),
              reads=reads, writes=writes)

    def cp(eng, out, in_, reads, writes):
        if eng == "act":
            S.add("act", lambda e: e.copy(out=out, in_=in_), reads=reads, writes=writes)
        else:
            S.add(eng, lambda e: e.tensor_copy(out=out, in_=in_), reads=reads, writes=writes)

    def red(eng, out, in_, reads, writes):
        S.add(eng, lambda e: e.tensor_reduce(out=out, in_=in_, axis=AX.X, op=ALU.add), reads=reads, writes=writes)

    def memset(eng, ap, val, writes):
        S.add(eng, lambda e: e.memset(ap, val), reads=(), writes=writes)

    wres = {"wi": [], "wo": [], "wg": [], "wd": []}

    def cast(name, dst, src, nparts):
        rows = src.shape[0] // nparts
        for i in range(nparts):
            r = (name, i)
            wres[name].append(r)
            dma("pool", dst[i * rows:(i + 1) * rows, :], src[i * rows:(i + 1) * rows, :], (), [r], r)

    cast("wi", wi_b, w_in_d, 4)
    for nm, dst, src in (("cf", cf, cf_d), ("cb", cb, cb_d), ("anw", anw, anw_d), ("fnw", fnw, fnw_d),
                         ("convw", convw, convw_d), ("alog", alog, alog_d),
                         ("dtb", dtb, dtb_d), ("gnw", gnw, gnw_d), ("rgw", rgw, rgw_d),
                         ("Sg", Sg, sg0_d), ("Sr", Sr, sr0_d), ("hist", hist, hist0_d)):
        dma("sp", dst[:], src[:], (), [nm], nm)
    cast("wo", wo_b, w_out_d, 2)
    cast("wg", wg_b, w_gu_d, 4)
    cast("wd", wd_b, w_dn_d, 4)
    dma("sp", wba[:], wi_v[:, :, OFF_BA:OFF_BA + 16], wres["wi"], ["wba"], "wba")
    act(nega[:], alog[:], AF.Exp, ["alog"], ["nega"])
    ts("dve", nega[:], nega[:], -1.0, None, ALU.mult, None, ["nega"], ["nega"])
    cp("act", Sgb[:], Sg[:], ["Sg"], ["Sgb"])
    cp("act", Srb[:], Sr[:], ["Sr"], ["Srb"])

    def rms_to_act16(nch, wcol, wname, first_load=None):
        memset("dve", ss4[:], 0.0, ["ss4"])
        for c in range(nch):
            xr = ("xt", c)
            for q in range(4):
                act(junk[:], xt[:, c, q * 512:(q + 1) * 512], AF.Square, [xr, "ss4"], ["junk", ("ss4", c, q)],
                    accum_out=ss4[:, c * 4 + q:c * 4 + q + 1])
            red("dve", ssum[:, c:c + 1], ss4[:, c * 4:c * 4 + 4], [("ss4", c, q) for q in range(4)], [("ssum", c)])
            act(rstd[:, c:c + 1], ssum[:, c:c + 1], AF.Ln, [("ssum", c)], [("rstd", c)], bias=RMS_EPS, scale=1.0 / D)
            act(rstd[:, c:c + 1], rstd[:, c:c + 1], AF.Exp, [("rstd", c)], [("rstd", c)], scale=-0.5)
            hbt = hb[0]
            hr = ("hb", 0)
            act(hbt[:], xt[:, c, :], AF.Copy, [xr, ("rstd", c)], [hr], scale=rstd[:, c:c + 1])
            for half in range(2):
                ptt, ptr = get_pt()
                tr_group([(ptt[:, i * 128:(i + 1) * 128], hbt[:, (half * 8 + i) * 128:(half * 8 + i + 1) * 128], ident_b)
                          for i in range(8)], [hr, "cb"], [ptr])
                tt("dve", act16[:, half * 8:(half + 1) * 8, c * 128:(c + 1) * 128], v3(ptt[:], 8),
                   bc_last(wcol[:, half * 8:(half + 1) * 8], 128), ALU.mult, [ptr, wname], [("hT", c, half)])

    def hT_res(nch):
        return [("hT", c, h) for c in range(nch) for h in range(2)]

    def load_w(view, c0, cw, k0, nk, srcres):
        slot, sres = get_ring()
        dma("sp", slot[:, 0:nk, 0:cw], view[:, k0:k0 + nk, c0:c0 + cw], srcres, [sres], sres)
        return slot, sres

    td_i = [0]

    def emit_tile(td):
        nch = td["nch"]
        T = nch * 128
        full = td["full"]
        L = td["L"]
        samp = td["samp"]
        x_d = td["x"]
        tok0 = td["tok0"]
        need_q = full or td["qhist"]
        kdc = kdc_samp if samp else kdc_full
        gl_r = td["gl_r"]
        td_i[0] += 1
        if td_i[0] % 4 == 1:
            S.epoch += 1
        S.mark('tile%d_A' % td_i[0])
        for c in range(nch):
            dma("sp", xt[:, c, :], x_d[c * 128:(c + 1) * 128, :], (), [("xt", c)], ("xt", c))
        dma("sp", cosT[:, 0:nch, :], cos_d[tok0:tok0 + T, :].rearrange("(c p) f -> p c f", p=128), (), ["cos"], "cos")
        dma("sp", sinT[:, 0:nch, :], sin_d[tok0:tok0 + T, :].rearrange("(c p) f -> p c f", p=128), (), ["sin"], "sin")
        rms_to_act16(nch, anw, "anw")
        HT = hT_res(nch)

        S.mark('tile%d_B0' % td_i[0])
        pool_sel[0] = "Z"
        S.begin_stream()
        pba, pbar = get_pa()
        for c in range(nch):
            mm_group([(pba[:, c * 16:(c + 1) * 16], act16[:, kc, c * 128:(c + 1) * 128], wba[:, kc, :], kc == 0, kc == KC - 1)
                      for kc in range(KC)], [("hT", c, 0), ("hT", c, 1), "wba"], [pbar] if c == nch - 1 else [pbar])
        bav = ba[:, 0:nch, :]
        cp("dve", bav, v3(pba[:, 0:nch * 16], nch), [pbar], ["ba"])
        e8, b8, x8 = tm8[0][:, 0:nch, :], rowsrc[:, 0:nch, 8:16], tm8[1][:, 0:nch, :]
        act(e8, ba[:, 0:nch, 0:8], AF.Exp, ["ba"], ["e8"], scale=-1.0)
        ts("dve", e8, e8, 1.0, None, ALU.add, None, ["e8"], ["e8"])
        S.add("dve", lambda e: e.reciprocal(out=b8, in_=e8), reads=["e8"], writes=["beta"])
        tt("dve", x8, ba[:, 0:nch, 8:16], bc_mid(dtb[:], nch), ALU.add, ["ba", "dtb"], ["x8"])
        act(x8, x8, AF.Exp, ["x8"], ["x8"])
        act(x8, x8, AF.Ln, ["x8"], ["x8"], bias=1.0, scale=1.0)
        gv = gtm[:, 0:nch, :]
        tt("dve", gv, x8, bc_mid(nega[:], nch), ALU.mult, ["x8", "nega"], ["g"])
        if samp:
            ts("dve", gv, gv, vm_samp[:, 0:1], None, ALU.mult, None, ["g", "cf"], ["g"])
            ts("dve", b8, b8, vm_samp[:, 0:1], None, ALU.mult, None, ["beta", "cf"], ["beta"])
        pgc, pgcr = get_pa()
        mm_group([(pgc[:, c * 8:(c + 1) * 8], mask_incl, gtm[:, c, :], True, True) for c in range(nch)] +
                 [(pgc[:, 32 + c * 8:32 + (c + 1) * 8], ones_f, gtm[:, c, :], True, True) for c in range(nch)],
                 ["g", "cf"], [pgcr])
        gcv = rowsrc[:, 0:nch, 0:8]
        cp("dve", gcv, v3(pgc[:, 0:nch * 8], nch), [pgcr], ["gc"])
        gtv = gtot[:, 0:nch, :]
        cp("dve", gtv, v3(pgc[:, 32:32 + nch * 8], nch), [pgcr], ["gtot"])
        eg = tm8[2][:, 0:nch, :]
        act(eg, gcv, AF.Exp, ["gc"], ["eg"])
        tt("dve", bge[:, 0:nch, :], b8, eg, ALU.mult, ["beta", "eg"], ["bge"])
        tt("dve", dks[:, 0:nch, :], gtv, gcv, ALU.subtract, ["gtot", "gc"], ["dks"])
        act(dks[:, 0:nch, :], dks[:, 0:nch, :], AF.Exp, ["dks"], ["dks"])
        act(glb[:, 0:nch, :], gtv, AF.Exp, ["gtot"], ["glb"])

        b0_stream = S.end_stream()
        pool_sel[0] = "A"
        S.mark('tile%d_B' % td_i[0])
        def conv_group(which, hh, h, pacc, paccr, i2):
            g = which * 8 + h
            pr = pre[i2]
            prr = ("pre", i2)
            cp("pool", pr[:, 0:3], hist[:, :, g], ["hist", ("hist", g)], [prr])
            cp("act", pr[:, 3:3 + T], pacc[:, 0:T], [paccr], [(prr, "b")])
            cp("pool", hist[:, :, g], pr[:, L:L + 3], [prr, (prr, "b")], [("hist", g)])
            return pr, [prr, (prr, "b")]


        def Kc(gi, name):
            return ("c%d" % gi, PHYS.get(name, name))

        def Bc(gi, name):
            return b512g[gi][PHYS.get(name, name)]

        def proj_head(which, hh, h, slot, sres):
            pacc, paccr = get_pa()
            mm_group([(pacc[:, 0:T], slot[:, kc, hh * 128:(hh + 1) * 128], act16[:, kc, 0:T], kc == 0, kc == KC - 1)
                      for kc in range(KC)], HT + [sres], [paccr])
            i2 = hh % 2
            pr, prres = conv_group(which, hh, h, pacc, paccr, i2)
            if which == 0 and not full:
                return
            g = which * 8 + h
            ca = cacc[i2]
            car = ("cacc", i2)
            act(ca[:, 0:T], pacc[:, 0:T], AF.Copy, [paccr, "convw"], [car], scale=convw[:, g, 3:4])
            for j in range(3):
                stt("dve", ca[:, 0:T], pr[:, j:j + T], convw[:, g, j:j + 1], ca[:, 0:T], ALU.mult, ALU.add,
                    prres + [car, "convw"], [car])
            if which == 2:
                act(vT[:, h, 0:T], ca[:, 0:T], AF.Silu, [car], [("vT", h)])
                return
            q_ = qs[i2]
            qr_ = ("qs", i2)
            act(q_[:, 0:T], ca[:, 0:T], AF.Silu, [car], [qr_])
            act(sqb[i2][:, 0:T], q_[:, 0:T], AF.Square, [qr_], [("sqb", i2)])
            pss, pssr = get_pa()
            mm_group([(pss[:, 0:T], ones_b, sqb[i2][:, 0:T], True, True)], [("sqb", i2), "cb"], [pssr])
            act(rsb[i2][:, 0:T], pss[:, 0:T], AF.Ln, [pssr], [("rsb", i2)], bias=L2_EPS, scale=1.0)
            act(rsb[i2][:, 0:T], rsb[i2][:, 0:T], AF.Exp, [("rsb", i2)], [("rsb", i2)], scale=-0.5)
            if which == 1:
                tt("dve", kTn[:, h, 0:T], q_[:, 0:T], rsb[i2][:, 0:T], ALU.mult, [qr_, ("rsb", i2)], [("kTn", h)])
            else:
                stt("dve", qTn[:, h, 0:T], q_[:, 0:T], 128.0 ** -0.5, rsb[i2][:, 0:T], ALU.mult, ALU.mult,
                    [qr_, ("rsb", i2)], [("qTn", h)])

        b1_list = []
        for gi in range(NH_G // HG):
            h0 = gi * HG
            for which, off, dst in ((1, OFF_K, kTn), (2, OFF_V, vT), (0, OFF_Q, qTn)):
                if which == 0 and not need_q:
                    continue
                S.begin_stream()
                slot, sres = load_w(wi_v, off + h0 * 128, HG * 128, 0, KC, wres["wi"])
                b1_list.extend(S.end_stream())
                hstreams = []
                for hh in range(HG):
                    h = h0 + hh
                    pool_sel[0] = "H%d" % (hh % 2)
                    S.begin_stream()
                    proj_head(which, hh, h, slot, sres)
                    hstreams.append(S.end_stream())
                    pool_sel[0] = "A"
                    if hh % 2 == 1:
                        b1_list.extend(S.merge(hstreams[hh - 1], hstreams[hh], emit=False))
            if full:
                pool_sel[0] = "H0"
                S.begin_stream()
                slot, sres = load_w(wi_v, OFF_Z + h0 * 128, HG * 128, 0, KC, wres["wi"])
                for c in range(nch):
                    pz, pzr = get_pa()
                    mm_group([(pz[:, :], act16[:, kc, c * 128:(c + 1) * 128], slot[:, kc, :], kc == 0, kc == KC - 1)
                              for kc in range(KC)], [("hT", c, 0), ("hT", c, 1), sres], [pzr])
                    ft, ftr = get_f()
                    act(ft[:], pz[:], AF.Silu, [pzr], [ftr])
                    tt("pool", v3(zg[:, c, h0 * 128:(h0 + HG) * 128], HG), v3(ft[:], HG), bc_mid(gnw[:], HG), ALU.mult,
                       [ftr, "gnw"], [("zg", c, gi)])
                b1_list.extend(S.end_stream())
                pool_sel[0] = "A"
        S.merge(b1_list, b0_stream)

        def gdn_chain(gi, c):
            h0 = gi * HG
            cs = slice(c * 128, (c + 1) * 128)
            hs = slice(h0, h0 + HG)
            K_ = lambda n: Kc(gi, n)
            B_ = lambda n: Bc(gi, n)
            KT = [("kTn", h0 + hh) for hh in range(HG)]
            QT = [("qTn", h0 + hh) for hh in range(HG)]
            VT = [("vT", h0 + hh) for hh in range(HG)]
            Dts_, Dti_ = Dtsg[gi], Dtig[gi]
            dg, dgr = get_f()
            tt("dve", v3(dg[:], HG), bc_mid(ident_f, HG), bc_last(rowsrc[:, c, h0:h0 + HG], 128), ALU.mult, ["gc", "cf"], [dgr])
            db, dbr = get_f()
            tt("dve", v3(db[:], HG), bc_mid(ident_f, HG), bc_last(rowsrc[:, c, 8 + h0:8 + h0 + HG], 128), ALU.mult,
               ["beta", "cf"], [dbr])
            pgb, pgbr = get_pa()
            pbb, pbbr = get_pa()
            mm_group([(pgb[:, r * 128:(r + 1) * 128], ones_f, dg[:, r * 128:(r + 1) * 128], True, True) for r in range(HG)],
                     [dgr, "cf"], [pgbr])
            mm_group([(pbb[:, r * 128:(r + 1) * 128], ones_f, db[:, r * 128:(r + 1) * 128], True, True) for r in range(HG)],
                     [dbr, "cf"], [pbbr])
            f0, f0r = get_f()
            tt("dve", v3(f0[:], HG), v3(pgb[:], HG), bc_last(rowsrc[:, c, h0:h0 + HG], 128), ALU.subtract, [pgbr, "gc"], [f0r])
            ts("dve", f0[:], f0[:], 0.0, None, ALU.min, None, [f0r], [f0r])
            act(f0[:], f0[:], AF.Exp, [f0r], [f0r])
            tt("pool", Dts_[:], v3(f0[:], HG), bc_mid(mask_strict, HG), ALU.mult, [f0r, "cf"], [K_("Dts")])
            if full:
                tt("pool", Dti_[:], v3(f0[:], HG), bc_mid(mask_incl, HG), ALU.mult, [f0r, "cf"], [K_("Dti")])
            tt("dve", B_("kb")[:], kTn[:, hs, cs], v3(pbb[:], HG), ALU.mult, KT + [pbbr], [K_("kb")])
            if full:
                act(B_("qd")[:], v3(pgb[:], HG), AF.Exp, [pgbr], [K_("qd")])
                tt("dve", B_("qd")[:], qTn[:, hs, cs], B_("qd")[:], ALU.mult, QT + [K_("qd")], [K_("qd")])
            pat, patr = get_pa()
            mm_group([(pat[:, hh * 128:(hh + 1) * 128], kTn[:, h0 + hh, cs], B_("kb")[:, hh, :], True, True) for hh in range(HG)],
                     KT + [K_("kb")], [patr])
            tt("dve", B_("AT")[:], v3(pat[:], HG), Dts_[:], ALU.mult, [patr, K_("Dts")], [K_("AT")])
            if full:
                pqk, pqkr = get_pa()
                mm_group([(pqk[:, hh * 128:(hh + 1) * 128], kTn[:, h0 + hh, cs], qTn[:, h0 + hh, cs], True, True) for hh in range(HG)],
                         KT + QT, [pqkr])
                tt("dve", B_("attnT")[:], v3(pqk[:], HG), Dti_[:], ALU.mult, [pqkr, K_("Dti")], [K_("attnT")])
            ptt, ptr = get_pt()
            tr_group([(ptt[:, hh * 128:(hh + 1) * 128], B_("AT")[:, hh, :], ident_b) for hh in range(HG)], [K_("AT"), "cb"], [ptr])
            cp("act", B_("A")[:], v3(ptt[:, 0:HG * 128], HG), [ptr], [K_("A")])
            tt("pool", B_("PT0")[:], B_("AT")[:], bc_mid(bd8, HG), ALU.mult, [K_("AT"), "cb"], [K_("PT0")])
            tt("pool", B_("P0")[:], B_("A")[:], bc_mid(bd8, HG), ALU.mult, [K_("A"), "cb"], [K_("P0")])
            tt("pool", B_("RT0")[:], bc_mid(ident_b, HG), B_("PT0")[:], ALU.subtract, [K_("PT0"), "cb"], [K_("RT0")])

            def mm4(lname, rname):
                p_, pr_ = get_pa()
                mm_group([(p_[:, hh * 128:(hh + 1) * 128], B_(lname)[:, hh, :], B_(rname)[:, hh, :], True, True)
                          for hh in range(HG)], [K_(lname), K_(rname)], [pr_])
                return p_, pr_
            pp, ppr = mm4("PT0", "P0")
            ppt, pptr = mm4("P0", "PT0")
            tt("dve", B_("IpP")[:], v3(pp[:], HG), bc_mid(ident_b, HG), ALU.add, [ppr, "cb"], [K_("IpP")])
            cp("act", B_("P1")[:], v3(pp[:], HG), [ppr], [K_("P1")])
            cp("act", B_("PT1")[:], v3(ppt[:], HG), [pptr], [K_("PT1")])
            prt_, prtr_ = mm4("IpP", "RT0")
            cp("act", B_("RT1")[:], v3(prt_[:], HG), [prtr_], [K_("RT1")])
            pp, ppr = mm4("PT1", "P1")
            tt("dve", B_("IpP")[:], v3(pp[:], HG), bc_mid(ident_b, HG), ALU.add, [ppr, "cb"], [K_("IpP")])
            prt_, prtr_ = mm4("IpP", "RT1")
            cp("act", B_("RT0")[:], v3(prt_[:], HG), [prtr_], [K_("RT0")])
            RTn = "RT0"
            for li in range(4):
                RTc = "RT1" if RTn == "RT0" else "RT0"
                tt("dve", B_("P0")[:], B_("A")[:], bc_mid(mX[li], HG), ALU.mult, [K_("A"), "cb"], [K_("P0")])
                ptt, ptr = get_pt()
                tr_group([(ptt[:, hh * 128:(hh + 1) * 128], B_(RTn)[:, hh, :], ident_b) for hh in range(HG)],
                         [K_(RTn), "cb"], [ptr])
                cp("act", B_("PT0")[:], v3(ptt[:, 0:HG * 128], HG), [ptr], [K_("PT0")])
                py_, pyr_ = mm4("P0", RTn)
                cp("act", B_("P1")[:], v3(py_[:], HG), [pyr_], [K_("P1")])
                pz_, pzr_ = mm4("PT0", "P1")
                tt("dve", B_(RTc)[:], B_(RTn)[:], v3(pz_[:], HG), ALU.subtract, [K_(RTn), pzr_], [K_(RTc)])
                RTn = RTc
            RT = B_(RTn)
            ptt, ptr = get_pt()
            tr_group([(ptt[:, hh * 128:(hh + 1) * 128], vT[:, h0 + hh, cs], ident_b) for hh in range(HG)], VT + ["cb"], [ptr])
            tt("dve", B_("vb")[:], v3(ptt[:, 0:HG * 128], HG), bc_last(rowsrc[:, c, 8 + h0:8 + h0 + HG], 128), ALU.mult,
               [ptr, "beta"], [K_("vb")])
            ptt, ptr = get_pt()
            tr_group([(ptt[:, hh * 128:(hh + 1) * 128], kTn[:, h0 + hh, cs], ident_b) for hh in range(HG)], KT + ["cb"], [ptr])
            tt("dve", B_("kbg")[:], v3(ptt[:, 0:HG * 128], HG), bc_last(bge[:, c, h0:h0 + HG], 128), ALU.mult,
               [ptr, "bge"], [K_("kbg")])
            tt("dve", B_("kdec")[:], v3(ptt[:, 0:HG * 128], HG), bc_last(dks[:, c, h0:h0 + HG], 128), ALU.mult,
               [ptr, "dks"], [K_("kdec")])
            pu0, pu0r = mm4(RTn, "vb")
            u0s, u0sr = get_f()
            cp("act", u0s[:], pu0[:], [pu0r], [u0sr])
            pkc, pkcr = mm4("kbg", RTn)
            cp("act", B_("kcdT")[:], v3(pkc[:], HG), [pkcr], [K_("kcdT")])
            SG = [("Sgb", h0 + hh) for hh in range(HG)]
            pw, pwr = get_pa()
            mm_group([(pw[:, hh * 128:(hh + 1) * 128], B_("kcdT")[:, hh, :], Sgb[:, h0 + hh, :], True, True) for hh in range(HG)],
                     [K_("kcdT"), "Sgb"] + SG, [pwr])
            tt("dve", B_("u")[:], v3(u0s[:], HG), v3(pw[:], HG), ALU.subtract, [u0sr, pwr], [K_("u")])
            if full:
                po, por = get_pa()
                items = []
                for hh in range(HG):
                    items.append((po[:, hh * 128:(hh + 1) * 128], B_("qd")[:, hh, :], Sgb[:, h0 + hh, :], True, False))
                    items.append((po[:, hh * 128:(hh + 1) * 128], B_("attnT")[:, hh, :], B_("u")[:, hh, :], False, True))
                mm_group(items, [K_("qd"), K_("attnT"), K_("u"), "Sgb"] + SG, [por])
                osb, osbr = get_f()
                cp("act", osb[:], po[:], [por], [osbr])
            pds, pdsr = mm4("kdec", "u")
            for hh in range(HG):
                h = h0 + hh
                stt("dve", Sg[:, h, :], Sg[:, h, :], glb[:, c, h:h + 1], pds[:, hh * 128:(hh + 1) * 128], ALU.mult, ALU.add,
                    ["Sg", ("Sg", h), "glb", pdsr], [("Sg", h)])
            cp("act", Sgb[:, h0:h0 + HG, :], Sg[:, h0:h0 + HG, :], ["Sg"] + [("Sg", h0 + hh) for hh in range(HG)], SG)
            if full:
                sq_, sqr_ = get_f()
                tt("pool", sq_[:], osb[:], osb[:], ALU.mult, [osbr], [sqr_])
                red("dve", st8[:, gi * 4:gi * 4 + HG], v3(sq_[:], HG), [sqr_], [K_("st8")])
                act(st8[:, gi * 4:gi * 4 + HG], st8[:, gi * 4:gi * 4 + HG], AF.Ln, [K_("st8")], [K_("st8")], bias=RMS_EPS, scale=1.0 / 128)
                act(st8[:, gi * 4:gi * 4 + HG], st8[:, gi * 4:gi * 4 + HG], AF.Exp, [K_("st8")], [K_("st8")], scale=-0.5)
                mixs = B_("mixs")
                tt("dve", mixs[:], v3(osb[:], HG), bc_last(st8[:, gi * 4:gi * 4 + HG], 128), ALU.mult, [osbr, K_("st8")], [K_("mixs")])
                tt("pool", mixs[:], mixs[:], v3(zg[:, c, h0 * 128:(h0 + HG) * 128], HG), ALU.mult, [K_("mixs"), ("zg", c, gi)], [K_("mixs")])
                ptt, ptr = get_pt()
                tr_group([(ptt[:, hh * 128:(hh + 1) * 128], mixs[:, hh, :], ident_b) for hh in range(HG)], [K_("mixs"), "cb"], [ptr])
                cp("act", mixT[:, h0:h0 + HG, cs], v3(ptt[:, 0:HG * 128], HG), [ptr], [("mixT", c, "g", gi)])

        for c in range(nch):
            streams = []
            for gi in range(NH_G // HG):
                pool_sel[0] = "G%d" % gi
                S.begin_stream()
                gdn_chain(gi, c)
                streams.append(S.end_stream())
            pool_sel[0] = "A"
            S.merge(streams[0], streams[1])

        S.mark('tile%d_C' % td_i[0])
        def rotary(psrc, psrcr, c, dst, dstr):
            xsb, xsbr = get_f()
            cp("act", xsb[:], psrc[:], [psrcr], [xsbr])
            xv = xsb[:].rearrange("p (h i two) -> p h i two", h=2, two=2)
            x0, x1 = xv[:, :, :, 0], xv[:, :, :, 1]
            dv = dst.rearrange("p (h i two) -> p h i two", h=2, two=2)
            cosb, sinb = bc_mid(cosT[:, c, :], 2), bc_mid(sinT[:, c, :], 2)
            t1, t1r = get_f()
            t2, t2r = get_f()
            a1, a2 = v3(t1[:, 0:256], 2), v3(t1[:, 256:512], 2)
            b1, b2 = v3(t2[:, 0:256], 2), v3(t2[:, 256:512], 2)
            tt("dve", a1, x0, cosb, ALU.mult, [xsbr, "cos"], [(t1r, 0)])
            tt("pool", a2, x1, sinb, ALU.mult, [xsbr, "sin"], [(t1r, 1)])
            tt("dve", dv[:, :, :, 0], a1, a2, ALU.subtract, [(t1r, 0), (t1r, 1)], [(dstr, 0)])
            tt("pool", b1, x1, cosb, ALU.mult, [xsbr, "cos"], [(t2r, 0)])
            tt("dve", b2, x0, sinb, ALU.mult, [xsbr, "sin"], [(t2r, 1)])
            tt("pool", dv[:, :, :, 1], b1, b2, ALU.add, [(t2r, 0), (t2r, 1)], [(dstr, 1)])
            return [(dstr, 0), (dstr, 1)]

        def ret_k(rp, c, slot, sres, rt, rtk):
            cs = slice(c * 128, (c + 1) * 128)
            pk, pkr = get_pa()
            mm_group([(pk[:, :], act16[:, kc, cs], slot[:, kc, :], kc == 0, kc == KC - 1) for kc in range(KC)],
                     [("hT", c, 0), ("hT", c, 1), sres], [pkr])
            rres = rotary(pk, pkr, c, rt[:], rtk)
            tt("dve", v3(kdr[:, c, :], 2), v3(rt[:], 2), bc_last(kdc[:, 2 * rp:2 * rp + 2], 256), ALU.mult,
               rres + ["cb"], [("kdr", c)])
            ptt, ptr = get_pt()
            tr_group([(ptt[:, i * 128:(i + 1) * 128], rt[:, i * 128:(i + 1) * 128], ident_b) for i in range(4)],
                     rres + ["cb"], [ptr])
            cp("act", krT[:, :, cs], v3(ptt[:, 0:512], 4), [ptr], [("krT", c)])

        def ret_v(rp, c, slot, sres):
            cs = slice(c * 128, (c + 1) * 128)
            pv, pvr = get_pa()
            mm_group([(pv[:, :], act16[:, kc, cs], slot[:, kc, :], kc == 0, kc == KC - 1) for kc in range(KC)],
                     [("hT", c, 0), ("hT", c, 1), sres], [pvr])
            cp("act", vr[:, c, :], pv[:], [pvr], [("vr", c)])

        def ret_q(rp, c, slot, sres, rt, rtk):
            cs = slice(c * 128, (c + 1) * 128)
            pq, pqr = get_pa()
            mm_group([(pq[:, :], act16[:, kc, cs], slot[:, kc, :], kc == 0, kc == KC - 1) for kc in range(KC)],
                     [("hT", c, 0), ("hT", c, 1), sres], [pqr])
            rres = rotary(pq, pqr, c, rt[:], rtk)
            ptt, ptr = get_pt()
            tr_group([(ptt[:, i * 128:(i + 1) * 128], rt[:, i * 128:(i + 1) * 128], ident_b) for i in range(4)],
                     rres + ["cb"], [ptr])
            cp("act", qrT[:, :, cs], v3(ptt[:, 0:512], 4), [ptr], [("qrT", c)])
            tt("dve", qdT[:, :, cs].rearrange("p (h two) i -> p h two i", h=2),
               ptt[:, 0:512].rearrange("p (h two i) -> p h two i", h=2, two=2),
               EBr[:, 2 * rp:2 * rp + 2, :].unsqueeze(2).to_broadcast([128, 2, 2, 128]), ALU.mult,
               [ptr, "cf"], [("qdT", c)])

        def ret_g(rp, c, slot, sres):
            cs = slice(c * 128, (c + 1) * 128)
            pg, pgr = get_pa()
            mm_group([(pg[:, :], act16[:, kc, cs], slot[:, kc, :], kc == 0, kc == KC - 1) for kc in range(KC)],
                     [("hT", c, 0), ("hT", c, 1), sres], [pgr])
            ft, ftr = get_f()
            act(ft[:], pg[:], AF.Silu, [pgr], [ftr])
            tt("pool", grw[:, c, :], ft[:], rgw[:, rp * 512:(rp + 1) * 512], ALU.mult, [ftr, "rgw"], [("grw", c)])

        for rp in range(2):
            pool_sel[0] = "H0"
            S.begin_stream()
            slot, sres = load_w(wi_v, OFF_KR + rp * 512, 512, 0, KC, wres["wi"])
            for c in range(nch):
                ret_k(rp, c, slot, sres, rot, "rot")
            if full:
                slot, sres = load_w(wi_v, OFF_VR + rp * 512, 512, 0, KC, wres["wi"])
                for c in range(nch):
                    ret_v(rp, c, slot, sres)
            sx = S.end_stream()
            pool_sel[0] = "H1"
            S.begin_stream()
            if full:
                slot, sres = load_w(wi_v, OFF_QR + rp * 512, 512, 0, KC, wres["wi"])
                for c in range(nch):
                    ret_q(rp, c, slot, sres, rot1, "rot1")
                slot, sres = load_w(wi_v, OFF_GR + rp * 512, 512, 0, KC, wres["wi"])
                for c in range(nch):
                    ret_g(rp, c, slot, sres)
            else:
                slot, sres = load_w(wi_v, OFF_VR + rp * 512, 512, 0, KC, wres["wi"])
                for c in range(nch):
                    ret_v(rp, c, slot, sres)
            sy = S.end_stream()
            pool_sel[0] = "A"
            S.merge(sx, sy)
            for c in range(nch):
                cs = slice(c * 128, (c + 1) * 128)
                SR = [("Srb", 2 * rp + hl) for hl in range(2)]
                if full:
                    pa_, par_ = get_pa()
                    items = []
                    for hl in range(2):
                        for half in range(2):
                            items.append((pa_[:, hl * 128:(hl + 1) * 128], krT[:, hl * 2 + half, cs], qrT[:, hl * 2 + half, cs],
                                          half == 0, half == 1))
                    mm_group(items, [("krT", c), ("qrT", c)], [par_])
                    tt("dve", attr[:], v3(pa_[:, 0:256], 2), DrT[:, 2 * rp:2 * rp + 2, :], ALU.mult, [par_, "cf"], ["attr"])
                    po, por = get_pa()
                    items = []
                    for hl in range(2):
                        h = 2 * rp + hl
                        oo = po[:, hl * 256:(hl + 1) * 256]
                        items.append((oo, qdT[:, hl * 2 + 0, cs], Srb[:, h, 0, :], True, False))
                        items.append((oo, qdT[:, hl * 2 + 1, cs], Srb[:, h, 1, :], False, False))
                        items.append((oo, attr[:, hl, :], vr[:, c, hl * 256:(hl + 1) * 256], False, True))
                    mm_group(items, [("qdT", c), "attr", ("vr", c), "Srb"] + SR, [por])
                    osb, osbr = get_f()
                    cp("act", osb[:], po[:], [por], [osbr])
                for hl in range(2):
                    h = 2 * rp + hl
                    pd, pdr = get_pa()
                    mm_group([(pd[:, half * 256:(half + 1) * 256], kdr[:, c, hl * 256 + half * 128:hl * 256 + (half + 1) * 128],
                               vr[:, c, hl * 256:(hl + 1) * 256], True, True) for half in range(2)],
                             [("kdr", c), ("vr", c)], [pdr])
                    stt("dve", Sr[:, h, :, :], Sr[:, h, :, :], float(gl_r[h]), v3(pd[:], 2), ALU.mult, ALU.add,
                        ["Sr", ("Sr", h), pdr], [("Sr", h)])
                    cp("act", Srb[:, h, :, :], Sr[:, h, :, :], ["Sr", ("Sr", h)], [("Srb", h)])
                if full:
                    sq_, sqr_ = get_f()
                    red("dve", st8[:, 8:10], v3(osb[:], 2), [osbr], ["st8b"])
                    tt("pool", sq_[:], osb[:], osb[:], ALU.mult, [osbr], [sqr_])
                    red("dve", st8[:, 10:12], v3(sq_[:], 2), [sqr_], ["st8c"])
                    ts("dve", st8[:, 8:10], st8[:, 8:10], 1.0 / 256, None, ALU.mult, None, ["st8b"], ["st8b"])
                    tt("dve", st8[:, 12:14], st8[:, 8:10], st8[:, 8:10], ALU.mult, ["st8b"], ["st8d"])
                    stt("dve", st8[:, 10:12], st8[:, 10:12], 1.0 / 256, st8[:, 12:14], ALU.mult, ALU.subtract,
                        ["st8c", "st8d"], ["st8c"])
                    act(st8[:, 10:12], st8[:, 10:12], AF.Ln, ["st8c"], ["st8c"], bias=GN_EPS, scale=1.0)
                    act(st8[:, 10:12], st8[:, 10:12], AF.Exp, ["st8c"], ["st8c"], scale=-0.5)
                    mixs = b512["mixr"]
                    mflat = mixs[:].rearrange("p a b -> p (a b)")
                    for hl in range(2):
                        ts("dve", mflat[:, hl * 256:(hl + 1) * 256], osb[:, hl * 256:(hl + 1) * 256], st8[:, 8 + hl:9 + hl],
                           st8[:, 10 + hl:11 + hl], ALU.subtract, ALU.mult, [osbr, "st8b", "st8c", "mixr"], [("mixr", hl)])
                    tt("pool", mflat, mflat, grw[:, c, :], ALU.mult,
                       ["mixr", ("mixr", 0), ("mixr", 1), ("grw", c)], ["mixr"])
                    ptt, ptr = get_pt()
                    tr_group([(ptt[:, i * 128:(i + 1) * 128], mixs[:, i, :], ident_b) for i in range(4)], ["mixr", "cb"], [ptr])
                    cp("act", mixT[:, 8 + rp * 4:8 + rp * 4 + 4, cs], v3(ptt[:, 0:512], 4), [ptr], [("mixT", c, "r", rp)])

        if full:
            MX = [("mixT", c, "g", gi) for c in range(nch) for gi in range(NH_G // HG)] + \
                 [("mixT", c, "r", rp) for c in range(nch) for rp in range(2)]
            S.mark('tile%d_D' % td_i[0])
            for nb in range(4):
                slot, sres = load_w(wo_v, nb * 512, 512, 0, KC, wres["wo"])
                for c in range(nch):
                    cs = slice(c * 128, (c + 1) * 128)
                    pw_, pwr_ = get_pa()
                    mm_group([(pw_[:, :], mixT[:, kc, cs], slot[:, kc, :], kc == 0, kc == KC - 1) for kc in range(KC)],
                             MX + [sres], [pwr_])
                    tt("dve", xt[:, c, nb * 512:(nb + 1) * 512], xt[:, c, nb * 512:(nb + 1) * 512], pw_[:], ALU.add,
                       [("xt", c), pwr_], [("xt", c)])
            rms_to_act16(nch, fnw, "fnw")
            HT2 = hT_res(nch)
            S.mark('tile%d_E' % td_i[0])
            for part in range(NPART):
                g0 = part * GPP
                done = 0
                while done < GPP:
                    ng = min(4, GPP - done)
                    sg_, sgr_ = load_w(wg_v, (g0 + done) * 128, ng * 128, 0, KC, wres["wg"])
                    su_, sur_ = load_w(wg_v, DFF + (g0 + done) * 128, ng * 128, 0, KC, wres["wg"])
                    fts = []
                    for gi_ in range(ng):
                        pg_, pgr_ = get_pa()
                        mm_group([(pg_[:, 0:T], sg_[:, kc, gi_ * 128:(gi_ + 1) * 128], act16[:, kc, 0:T], kc == 0, kc == KC - 1)
                                  for kc in range(KC)], HT2 + [sgr_], [pgr_])
                        ft, ftr = get_f()
                        act(ft[:, 0:T], pg_[:, 0:T], AF.Silu, [pgr_], [ftr])
                        fts.append((ft, ftr))
                    for gi_ in range(ng):
                        ft, ftr = fts[gi_]
                        pu_, pur_ = get_pa()
                        mm_group([(pu_[:, 0:T], su_[:, kc, gi_ * 128:(gi_ + 1) * 128], act16[:, kc, 0:T], kc == 0, kc == KC - 1)
                                  for kc in range(KC)], HT2 + [sur_], [pur_])
                        tt("dve", actT[:, done + gi_, 0:T], ft[:, 0:T], pu_[:, 0:T], ALU.mult, [ftr, pur_], [("actT", done + gi_)])
                    done += ng
                AT_ = [("actT", i) for i in range(GPP)]
                for nb in range(4):
                    sd_, sdr_ = load_w(wd_v, nb * 512, 512, g0, GPP, wres["wd"])
                    for c in range(nch):
                        cs = slice(c * 128, (c + 1) * 128)
                        pd_, pdr_ = get_pa()
                        mm_group([(pd_[:, :], actT[:, i, cs], sd_[:, i, :], i == 0, i == GPP - 1) for i in range(GPP)],
                                 AT_ + [sdr_], [pdr_])
                        tt("dve", xt[:, c, nb * 512:(nb + 1) * 512], xt[:, c, nb * 512:(nb + 1) * 512], pd_[:], ALU.add,
                           [("xt", c), pdr_], [("xt", c)])
            S.mark('tile%d_F' % td_i[0])
            MXALL = [("mixT", c_, "g", gi_) for c_ in range(NCH) for gi_ in range(NH_G // HG)] + \
                    [("mixT", c_, "r", rp_) for c_ in range(NCH) for rp_ in range(2)]
            finw_v = mixT[:].rearrange("p a b -> p (a b)").bitcast(F32)
            dma("sp", finw_v, finw_d[:], (), MXALL + ["finw"], "finw")
            memset("dve", ss4[:], 0.0, ["ss4"])
            for c in range(nch):
                xr = ("xt", c)
                for q in range(4):
                    act(junk[:], xt[:, c, q * 512:(q + 1) * 512], AF.Square, [xr, "ss4"], ["junk", ("ss4", c, q)],
                        accum_out=ss4[:, c * 4 + q:c * 4 + q + 1])
                red("dve", ssum[:, c:c + 1], ss4[:, c * 4:c * 4 + 4], [("ss4", c, q) for q in range(4)], [("ssum", c)])
                act(rstd[:, c:c + 1], ssum[:, c:c + 1], AF.Ln, [("ssum", c)], [("rstd", c)], bias=RMS_EPS, scale=1.0 / D)
                act(rstd[:, c:c + 1], rstd[:, c:c + 1], AF.Exp, [("rstd", c)], [("rstd", c)], scale=-0.5)
                stt("dve", xt[:, c, :], xt[:, c, :], rstd[:, c:c + 1], finw_v, ALU.mult, ALU.mult, [xr, ("rstd", c), "finw"] + MXALL, [xr])
                if samp:
                    dma("pool", td["y"][0:DEC_SEQ, :], xt[0:DEC_SEQ, c, :], [xr], [("yout", c)], ("yst", c))
                else:
                    dma("pool", td["y"][c * 128:(c + 1) * 128, :], xt[:, c, :], [xr], [("yout", c)], ("yst", c))

    def store_states(oc, og, orr, tag):
        S.mark('store_' + tag)
        allSg = ["Sg"] + [("Sg", h) for h in range(NH_G)]
        allSr = ["Sr"] + [("Sr", h) for h in range(NH_R)]
        allH = ["hist"] + [("hist", g) for g in range(24)]
        dma("pool", og.rearrange("h k v -> k h v"), Sg[:], allSg, [("og", tag)], ("og", tag))
        dma("pool", orr.rearrange("h (two p) v -> p h two v", p=128), Sr[:], allSr, [("or", tag)], ("or", tag))
        ph, phr = get_pa()
        tr_group([(ph[0:72, 0:128], hist[:].rearrange("p j g -> p (j g)"), ident_f)], allH + ["cf"], [phr])
        cp("act", h72[:], ph[0:72, 0:128], [phr], ["h72"])
        for j in range(3):
            dma("pool", oc[j, :].rearrange("(g p) -> g p", p=128), h72[j * 24:(j + 1) * 24, :], ["h72"],
                [("oc", tag, j)], ("oc", tag, j))

    def reset_states():
        allSg = ["Sg"] + [("Sg", h) for h in range(NH_G)]
        allSr = ["Sr"] + [("Sr", h) for h in range(NH_R)]
        allH = ["hist"] + [("hist", g) for g in range(24)]
        memset("pool", Sg[:], 0.0, allSg)
        memset("pool", Sr[:], 0.0, allSr)
        memset("pool", hist[:], 0.0, allH)
        memset("dve", Sgb[:], 0.0, ["Sgb"] + [("Sgb", h) for h in range(NH_G)])
        memset("dve", Srb[:], 0.0, ["Srb"] + [("Srb", h) for h in range(NH_R)])

    lg = [math.log(1.0 - 2.0 ** (-5.0 - h)) for h in range(NH_R)]
    gl_full = [math.exp(l * C) for l in lg]
    gl_samp = [math.exp(l * DEC_SEQ) for l in lg]

    emit_tile(dict(nch=1, full=True, L=DEC_SEQ, samp=True, x=xs_d, tok0=0, qhist=False, gl_r=gl_samp, y=ys_d))
    store_states(ocs_d, ogs_d, ors_d, "s")
    reset_states()
    for t in range(n_pre):
        emit_tile(dict(nch=NCH, full=False, L=TT, samp=False, x=xp_d[t * TT:(t + 1) * TT, :], tok0=128 + t * TT,
                       qhist=(t == n_pre - 1), gl_r=gl_full, y=None))
    for t in range(n_own):
        emit_tile(dict(nch=NCH, full=True, L=TT, samp=False, x=xo_d[t * TT:(t + 1) * TT, :], tok0=128 + (n_pre + t) * TT,
                       qhist=False, gl_r=gl_full, y=yo_d[t * TT:(t + 1) * TT, :]))
    store_states(ocp_d, ogp_d, orp_d, "p")
    outs = [("og", "s"), ("or", "s"), ("og", "p"), ("or", "p")] + [("oc", tg, j) for tg in "sp" for j in range(3)] + \
           [("yout", c) for c in range(NCH)]
    S.add("pool", None, reads=outs, writes=())

    if trunc is not None:
        S.truncate(trunc)
    build_program.marks = S.marks
    build_program.sb_bytes = sb_bytes[0]
    build_program.lines = [(o.id, o.eng, getattr(o, 'line', -1)) for o in S.all]
    semkeys = S.finalize()
    build_program.nsem = len(semkeys)
    sems = {k: es.enter_context(nc.semaphore("s%d" % i)) for i, k in enumerate(semkeys)}
    with nc.Block() as block:
        def run(eng_name, e):
            for op in S.ops[eng_name]:
                for k, v in op.waits:
                    e.wait_ge(sems[k], v)
                if op.fn is None:
                    continue
                inst = op.fn(e)
                if op.signal:
                    k, _ = op.sigval
                    inst.then_inc(sems[k], 16 if op.dma is not None else 1)

        @block.tensor
        def _(e):
            run("pe", e)

        @block.scalar
        def _(e):
            run("act", e)

        @block.vector
        def _(e):
            run("dve", e)

        @block.gpsimd
        def _(e):
            run("pool", e)

        @block.sync
        def _(e):
            run("sp", e)
    es.close()
    return nc


CF_COLS = 128 * 4 + 512 + 512 + 4 + 4 + 1 + 16 + 128
CB_COLS = 256 + 5 * 128 + 8


def _consts():
    cf = np.zeros((128, CF_COLS), np.float32)
    j = np.arange(128)[:, None]
    i = np.arange(128)[None, :]
    o = 0
    cf[:, o:o + 128] = (i >= j); o += 128
    cf[:, o:o + 128] = (i > j); o += 128
    cf[:, o:o + 128] = np.eye(128); o += 128
    cf[:, o:o + 128] = 1.0; o += 128
    lg = np.log(1.0 - 2.0 ** (-5.0 - np.arange(NH_R, dtype=np.float64)))
    for h in range(NH_R):
        cf[:, o + h * 128:o + (h + 1) * 128] = np.where(i >= j, np.exp(lg[h] * (i - j)) * (256.0 ** -0.5), 0.0)
    o += 512
    for h in range(NH_R):
        cf[:, o + h * 128:o + (h + 1) * 128] = np.exp(lg[h] * (i + 1.0))
    o += 512
    jj = np.arange(128)
    for h in range(NH_R):
        cf[:, o + h] = np.exp(lg[h] * (C - 1 - jj)) * (256.0 ** -0.5)
    o += 4
    for h in range(NH_R):
        cf[:, o + h] = np.where(jj < DEC_SEQ, np.exp(lg[h] * (DEC_SEQ - 1 - np.minimum(jj, DEC_SEQ - 1))) * (256.0 ** -0.5), 0.0)
    o += 4
    cf[:, o] = (jj < DEC_SEQ); o += 1
    cf[0:16, o:o + 16] = np.eye(16); o += 16
    cf[0:16, o:o + 128] = 1.0; o += 128
    assert o == CF_COLS
    cb = np.zeros((128, CB_COLS), np.float32)
    cb[:, 0:128] = np.eye(128)
    cb[:, 128:256] = 1.0
    ii = np.arange(128)[:, None]
    jj2 = np.arange(128)[None, :]
    bd = lambda b: ((ii // b) == (jj2 // b)).astype(np.float32)
    cb[:, 256:384] = bd(8)
    for li, b in enumerate((8, 16, 32, 64)):
        cb[:, 384 + li * 128:512 + li * 128] = bd(2 * b) - bd(b)
    cb[:, 896:904] = cf[:, 128 * 4 + 1024:128 * 4 + 1032]
    return cf, cb.astype(ml_dtypes.bfloat16)


def _rope_tables(pos):
    d2 = 128
    inv = (1.0 / (10000.0 ** np.linspace(0.0, 1.0, d2, dtype=np.float32))).astype(np.float32)
    ang = pos.astype(np.float32)[:, None] * inv[None, :]
    return np.cos(ang).astype(np.float32), np.sin(ang).astype(np.float32)


_PROG_CACHE = {}


def run_cores(inputs, n_pre, n_own, seq_half):
    f32 = np.float32
    x_prompt = np.asarray(inputs["x_prompt"], f32)
    x_sample = np.asarray(inputs["x_sample"], f32)
    cf, cb = _consts()
    key = (n_pre, n_own)
    if key not in _PROG_CACHE:
        _PROG_CACHE[key] = build_program(n_pre, n_own)
    nc = _PROG_CACHE[key]
    w_in = np.ascontiguousarray(inputs["w_in"][0], f32)
    w_out = np.ascontiguousarray(inputs["w_out"][0], f32)
    w_gu = np.ascontiguousarray(inputs["w_gate_up"][0], f32)
    w_dn = np.ascontiguousarray(inputs["w_down"][0], f32)
    col = lambda v: np.ascontiguousarray(np.asarray(v, f32).reshape(KC, 128).T)
    bcast = lambda v: np.ascontiguousarray(np.broadcast_to(np.asarray(v, f32)[None, :], (128, v.shape[-1])))
    common = dict(
        w_in=w_in, w_out=w_out, w_gu=w_gu, w_dn=w_dn,
        anw=col(inputs["attn_norm_w"][0]), fnw=col(inputs["ffn_norm_w"][0]), finw=bcast(np.asarray(inputs["final_norm_w"])),
        convw=np.ascontiguousarray(np.asarray(inputs["conv_w"][0], f32).reshape(4, 24, 128).transpose(2, 1, 0)),
        alog=bcast(np.asarray(inputs["a_log"][0])), dtb=bcast(np.asarray(inputs["dt_bias"][0])),
        gnw=bcast(np.asarray(inputs["gdn_norm_w"][0])), rgw=bcast(np.asarray(inputs["ret_gn_w"][0])),
        cf32=cf, cbf16=cb,
    )
    in_maps = []
    for c in range(8):
        b, half = c // 2, c % 2
        xs = np.zeros((128, D), f32)
        xs[:DEC_SEQ] = x_sample[c]
        xo = np.ascontiguousarray(x_prompt[b, half * seq_half:(half + 1) * seq_half])
        if half == 1:
            xp = np.ascontiguousarray(x_prompt[b, 0:seq_half])
        else:
            xp = np.zeros((seq_half, D), f32)
        pos = np.concatenate([PAST_LEN + np.arange(128), np.arange(seq_half), half * seq_half + np.arange(seq_half)])
        cosd, sind = _rope_tables(pos)
        sg0 = np.ascontiguousarray(np.asarray(inputs["state_gdn"][0, c], f32).transpose(1, 0, 2))
        sr0 = np.ascontiguousarray(np.asarray(inputs["state_ret"][0, c], f32).reshape(NH_R, 2, 128, 256).transpose(2, 0, 1, 3))
        hist0 = np.ascontiguousarray(np.asarray(inputs["state_gdn_conv"][0, c], f32).reshape(3, 24, 128).transpose(2, 0, 1))
        m = dict(common)
        m.update(xs=xs, xp=xp, xo=xo, cosd=cosd, sind=sind, sg0=sg0, sr0=sr0, hist0=hist0)
        in_maps.append(m)
    res = run_bass_kernel_spmd(nc, in_maps, core_ids=list(range(8)))
    return res.results


def kernel(x_prompt, x_sample, state_gdn_conv, state_gdn, state_ret, attn_norm_w, w_in, conv_w, a_log, dt_bias,
           gdn_norm_w, ret_gn_w, w_out, ffn_norm_w, w_gate_up, w_down, final_norm_w):
    inputs = dict(x_prompt=x_prompt, x_sample=x_sample, state_gdn_conv=state_gdn_conv, state_gdn=state_gdn,
                  state_ret=state_ret, attn_norm_w=attn_norm_w, w_in=w_in, conv_w=conv_w, a_log=a_log, dt_bias=dt_bias,
                  gdn_norm_w=gdn_norm_w, ret_gn_w=ret_gn_w, w_out=w_out, ffn_norm_w=ffn_norm_w, w_gate_up=w_gate_up,
                  w_down=w_down, final_norm_w=final_norm_w)
    inputs = {k: np.asarray(v) for k, v in inputs.items()}
    B, SEQ = inputs["x_prompt"].shape[0], inputs["x_prompt"].shape[1]
    seq_half = SEQ // 2
    n_t = seq_half // TT
    r = run_cores(inputs, n_t, n_t, seq_half)
    f32 = np.float32
    y_prompt = np.zeros((B, SEQ, D), f32)
    for c in range(8):
        b, half = c // 2, c % 2
        y_prompt[b, half * seq_half:(half + 1) * seq_half] = r[c]["y_o"]
    y_sample = np.stack([r[c]["y_s"] for c in range(8)]).astype(f32)
    pc = np.stack([r[2 * b + 1]["oc_p"] for b in range(B)])[None].astype(f32)
    pg = np.stack([r[2 * b + 1]["og_p"] for b in range(B)])[None].astype(f32)
    pr = np.stack([r[2 * b + 1]["or_p"] for b in range(B)])[None].astype(f32)
    sc = np.stack([r[c]["oc_s"] for c in range(8)])[None].astype(f32)
    sg = np.stack([r[c]["og_s"] for c in range(8)])[None].astype(f32)
    sr = np.stack([r[c]["or_s"] for c in range(8)])[None].astype(f32)
    return (y_prompt, y_sample, pc, pg, pr, sc, sg, sr)
```
